# Optimizing a Trainium2 kernel written in Bass

```python
import jax, jax.numpy as jnp
from jax import lax
import numpy as np

D_MODEL = 1024
BATCH = 8
SEQ = 8192
DEPTH = 4

N_A = DEPTH // 2
N_B = DEPTH - N_A
MEM_LEN = 256
MEM_HEADS = 4
MEM_HEAD_DIM = D_MODEL // 8
MEM_WIDTH = MEM_HEADS * MEM_HEAD_DIM
CONV_DIM = D_MODEL
CONV_WIDTH = 3
FOX_HEADS = 8
FOX_HEAD_DIM = D_MODEL // FOX_HEADS
FOX_WIDTH = FOX_HEADS * FOX_HEAD_DIM
D_FF = ((8 * D_MODEL // 3 + 255) // 256) * 256
Q_BLOCK = 128
RMS_EPS = 1e-6
NEG_INF = float(np.finfo(np.float32).min)

kernel_name = 'yoco_shortconv_fox_macaron_memory'


def rmsnorm(x, g):
    xf = x.astype(jnp.float32)
    y = xf * lax.rsqrt(jnp.mean(xf * xf, axis=-1, keepdims=True) + RMS_EPS)
    return (y * g.astype(jnp.float32)).astype(x.dtype)


def swiglu(h, w_gate_up, w_down):
    gu = h @ w_gate_up
    gate, up = gu[..., :D_FF], gu[..., D_FF:]
    return (jax.nn.silu(gate) * up) @ w_down


def causal_depthwise_conv(u, w):
    c = u.shape[-1]
    return lax.conv_general_dilated(
        u, w[:, None, :].astype(u.dtype), window_strides=(1,),
        padding=[(CONV_WIDTH - 1, 0)],
        dimension_numbers=('NWC', 'WIO', 'NWC'),
        feature_group_count=c)


def memory_attention(q, mk, mv):
    b, s, h, dh = q.shape
    logits = jnp.einsum('bshd,bmhd->bhsm', q, mk,
                        preferred_element_type=jnp.float32) * (dh ** -0.5)
    p = jax.nn.softmax(logits, axis=-1).astype(mv.dtype)
    return jnp.einsum('bhsm,bmhd->bshd', p, mv).reshape(b, s, h * dh)


def forgetting_attention(q, k, v, logf_cum):
    b, s, h, dh = q.shape
    nb = s // Q_BLOCK
    qb = q.reshape(b, nb, Q_BLOCK, h, dh).transpose(1, 0, 2, 3, 4)
    cb = logf_cum.reshape(b, h, nb, Q_BLOCK).transpose(2, 0, 1, 3)
    k_pos = jnp.arange(s)
    scale = dh ** -0.5

    def block(args):
        q_blk, c_blk, i = args
        q_pos = i * Q_BLOCK + jnp.arange(Q_BLOCK)
        logits = jnp.einsum('bqhd,bkhd->bhqk', q_blk, k,
                            preferred_element_type=jnp.float32) * scale
        logits = logits + c_blk[..., :, None] - logf_cum[:, :, None, :]
        causal = k_pos[None, :] <= q_pos[:, None]
        logits = jnp.where(causal, logits, NEG_INF)
        p = jax.nn.softmax(logits, axis=-1).astype(v.dtype)
        return jnp.einsum('bhqk,bkhd->bqhd', p, v)

    out = lax.map(block, (qb, cb, jnp.arange(nb)))
    return out.transpose(1, 0, 2, 3, 4).reshape(b, s, h * dh)


def setup_inputs(seed: int = 0) -> dict:
    key = jax.random.key(seed)
    ks = jax.random.split(key, 20)
    f32 = jnp.float32

    def w(k, shape, fan_in):
        return jax.random.normal(k, shape, f32) * (fan_in ** -0.5)

    def gain(k, shape):
        return 1.0 + 0.02 * jax.random.normal(k, shape, f32)

    a_in_cols = 3 * CONV_DIM + MEM_WIDTH
    b_in_cols = FOX_WIDTH + MEM_WIDTH
    return {
        'x': jax.random.normal(ks[0], (BATCH, SEQ, D_MODEL), f32),
        'mem': jax.random.normal(ks[1], (BATCH, MEM_LEN, D_MODEL), f32),
        'ffn_norm': gain(ks[2], (DEPTH, 2, D_MODEL)),
        'ffn_w_gate_up': w(ks[3], (DEPTH, 2, D_MODEL, 2 * D_FF), D_MODEL),
        'ffn_w_down': w(ks[4], (DEPTH, 2, D_FF, D_MODEL), D_FF),
        'mix_norm': gain(ks[5], (DEPTH, D_MODEL)),
        'mem_norm': gain(ks[6], (D_MODEL,)),
        'mem_w_kv': w(ks[7], (DEPTH, D_MODEL, 2 * MEM_WIDTH), D_MODEL),
        'a_w_in': w(ks[8], (N_A, D_MODEL, a_in_cols), D_MODEL),
        'a_conv_w': w(ks[9], (N_A, CONV_WIDTH, CONV_DIM), CONV_WIDTH),
        'a_w_out': w(ks[10], (N_A, CONV_DIM + MEM_WIDTH, D_MODEL), CONV_DIM + MEM_WIDTH),
        'kv_norm': gain(ks[11], (D_MODEL,)),
        'w_kvf': w(ks[12], (D_MODEL, 2 * FOX_WIDTH + FOX_HEADS), D_MODEL),
        'b_f': jax.random.uniform(ks[13], (FOX_HEADS,), f32, minval=1.0, maxval=6.0),
        'b_w_q': w(ks[14], (N_B, D_MODEL, b_in_cols), D_MODEL),
        'b_w_out': w(ks[15], (N_B, FOX_WIDTH + MEM_WIDTH, D_MODEL), FOX_WIDTH + MEM_WIDTH),
        'final_norm': gain(ks[16], (D_MODEL,)),
    }


def reference(x, mem, ffn_norm, ffn_w_gate_up, ffn_w_down, mix_norm, mem_norm,
              mem_w_kv, a_w_in, a_conv_w, a_w_out, kv_norm, w_kvf, b_f,
              b_w_q, b_w_out, final_norm):
    b, s, _ = x.shape
    m = mem.shape[1]
    mem_n = rmsnorm(mem, mem_norm)
    k_sh = v_sh = c_sh = None

    for l in range(DEPTH):
        if l == N_A:
            hkv = rmsnorm(x, kv_norm)
            kvf = hkv @ w_kvf
            k_sh = kvf[..., :FOX_WIDTH].reshape(b, s, FOX_HEADS, FOX_HEAD_DIM)
            v_sh = kvf[..., FOX_WIDTH:2 * FOX_WIDTH].reshape(b, s, FOX_HEADS, FOX_HEAD_DIM)
            f_logit = (kvf[..., 2 * FOX_WIDTH:] + b_f).astype(jnp.float32)
            c_sh = jnp.cumsum(jax.nn.log_sigmoid(f_logit), axis=1).transpose(0, 2, 1)

        x = x + 0.5 * swiglu(rmsnorm(x, ffn_norm[l, 0]), ffn_w_gate_up[l, 0], ffn_w_down[l, 0])

        h = rmsnorm(x, mix_norm[l])
        mkv = mem_n @ mem_w_kv[l]
        mk = mkv[..., :MEM_WIDTH].reshape(b, m, MEM_HEADS, MEM_HEAD_DIM)
        mv = mkv[..., MEM_WIDTH:].reshape(b, m, MEM_HEADS, MEM_HEAD_DIM)
        if l < N_A:
            i = l
            proj = h @ a_w_in[i]
            gate_b = proj[..., :CONV_DIM]
            gate_c = proj[..., CONV_DIM:2 * CONV_DIM]
            u = proj[..., 2 * CONV_DIM:3 * CONV_DIM]
            qm = proj[..., 3 * CONV_DIM:].reshape(b, s, MEM_HEADS, MEM_HEAD_DIM)
            y_tok = gate_b * causal_depthwise_conv(gate_c * u, a_conv_w[i])
            y_mem = memory_attention(qm, mk, mv)
            x = x + jnp.concatenate([y_tok, y_mem], axis=-1) @ a_w_out[i]
        else:
            j = l - N_A
            proj = h @ b_w_q[j]
            q = proj[..., :FOX_WIDTH].reshape(b, s, FOX_HEADS, FOX_HEAD_DIM)
            qm = proj[..., FOX_WIDTH:].reshape(b, s, MEM_HEADS, MEM_HEAD_DIM)
            y_tok = forgetting_attention(q, k_sh, v_sh, c_sh)
            y_mem = memory_attention(qm, mk, mv)
            x = x + jnp.concatenate([y_tok, y_mem], axis=-1) @ b_w_out[j]

        x = x + 0.5 * swiglu(rmsnorm(x, ffn_norm[l, 1]), ffn_w_gate_up[l, 1], ffn_w_down[l, 1])

    return rmsnorm(x, final_norm)
```

```python
import numpy as np
from contextlib import ExitStack
import concourse.bass as bass
import concourse.mybir as mybir
from concourse.bass_utils import run_bass_kernel_spmd

F32 = mybir.dt.float32
BF16 = mybir.dt.bfloat16
AF = mybir.ActivationFunctionType
ALU = mybir.AluOpType

D = 1024
KC = 8
DFF = 2816
FC = 22
T = 512
NSUB = 4
H = 8
MH = 4
M = 256
DEPTH = 4
N_A = 2
EPS = 1e-6
SCALE = 128 ** -0.5
MASKV = -30000.0
ENGS = ("sync", "scalar", "vector", "gpsimd", "tensor")


class Ev:
    __slots__ = ("sem", "val")

    def __init__(self, sem, val):
        self.sem = sem
        self.val = val


class Buf:
    def __init__(self, t, slot=None):
        self.t = t
        self.w = None
        self.r = []
        self.slot = slot

    def __getitem__(self, k):
        return self.t[k]


class Prog:
    def __init__(self, nc, es):
        self.nc = nc
        self.es = es
        self.sem = {e: es.enter_context(nc.semaphore("c_" + e)) for e in ENGS}
        self.cnt = {e: 0 for e in ENGS}
        self.ops = {e: [] for e in ENGS}
        self.waited = {e: {} for e in ENGS}
        self.lazy = {e: [] for e in ENGS}
        self.last = {e: None for e in ENGS}
        self.dma_evs = []
        self.free_slots = {e: [] for e in ENGS}
        self.used_slots = {e: [] for e in ENGS}
        self.nslots = 0
        self.stats = {}

    def slot(self, eng):
        if self.free_slots[eng]:
            s = self.free_slots[eng].pop()
        else:
            self.nslots += 1
            s = [self.es.enter_context(self.nc.semaphore("d%s%d" % (eng[0], self.nslots))), 0]
        self.used_slots[eng].append(s)
        return s

    def _deps(self, R, W, deps, eng=None):
        dl = [d for d in deps if d is not None]
        own = self.sem.get(eng)
        for b in R:
            if b.w is not None:
                dl.append(b.w)
        for b in W:
            if b.w is not None and b.w.sem is not own:
                dl.append(b.w)
            dl.extend(r for r in b.r if r.sem is not own)
        return dl

    def op(self, eng, fn, R=(), W=(), deps=(), signal=True):
        dl = self._deps(R, W, deps, eng)
        ev = Ev(self.sem[eng], None)
        if signal:
            self.cnt[eng] += 1
            ev.val = self.cnt[eng]
            for l in self.lazy[eng]:
                l.val = ev.val
            self.lazy[eng] = []
            self.last[eng] = ev
        else:
            self.lazy[eng].append(ev)
        for b in R:
            b.r.append(ev)
        for b in W:
            b.w = ev
            b.r = []
        self.ops[eng].append((dl, fn, ev if signal else None))
        return ev

    def dma(self, eng, out, in_, slotbuf, R=(), W=(), deps=(), noncontig=False):
        dl = self._deps(R, W, deps)
        if slotbuf.slot is None:
            slotbuf.slot = {}
        if eng not in slotbuf.slot:
            slotbuf.slot[eng] = self.slot(eng)
        sl = slotbuf.slot[eng]
        sl[1] += 16
        ev = Ev(sl[0], sl[1])
        for b in R:
            b.r.append(ev)
        for b in W:
            b.w = ev
            b.r = []
        sem = sl[0]

        nc = self.nc

        def fn(e, out=out, in_=in_, sem=sem):
            if noncontig:
                with nc.allow_non_contiguous_dma(reason="tiny strided gather"):
                    e.dma_start(out=out, in_=in_).then_inc(sem, 16)
            else:
                e.dma_start(out=out, in_=in_).then_inc(sem, 16)
        self.ops[eng].append((dl, fn, None))
        self.dma_evs.append(ev)
        return ev

    def barrier(self):
        for e in ENGS:
            assert not self.lazy[e], "unsignaled tail on " + e
        best = {}
        for ev in [self.last[e] for e in ENGS if self.last[e] is not None] + self.dma_evs:
            k = id(ev.sem)
            if k not in best or best[k].val < ev.val:
                best[k] = ev
        evs = list(best.values())
        for e in ENGS:
            self.ops[e].append((evs, None, None))
        self.dma_evs = []
        for e in ENGS:
            self.free_slots[e].extend(self.used_slots[e])
            self.used_slots[e] = []

    def _run(self, eng, e):
        own = self.sem[eng]
        wd = self.waited[eng]
        st = self.stats.setdefault(eng, [0, 0])
        for (dl, fn, ev) in self.ops[eng]:
            for d in dl:
                if eng == "tensor" and d.sem is own:
                    continue
                assert d.val is not None
                k = id(d.sem)
                if wd.get(k, 0) >= d.val:
                    continue
                wd[k] = d.val
                e.wait_ge(d.sem, d.val)
                st[1] += 1
            if fn is not None:
                st[0] += 1
                ins = fn(e)
                if ev is not None:
                    ins.then_inc(ev.sem, 1)
        self.ops[eng] = []

    def emit(self):
        with self.nc.Block() as block:
            @block.sync
            def _(e):
                self._run("sync", e)

            @block.scalar
            def _(e):
                self._run("scalar", e)

            @block.vector
            def _(e):
                self._run("vector", e)

            @block.gpsimd
            def _(e):
                self._run("gpsimd", e)

            @block.tensor
            def _(e):
                self._run("tensor", e)


class Builder:
    def __init__(self, S, stop_after=None):
        self.S = S
        self.NT = S // T
        self.NCH = S // 128
        self.stop_after = stop_after
        self.nc = bass.Bass("TRN2", target_bir_lowering=False)
        self.phase_no = 0

    def sb(self, name, shape, dt):
        self.uid += 1
        return Buf(self.pes.enter_context(self.nc.sbuf_tensor("%s_%d" % (name, self.uid), shape, dt)))

    def ps(self, name, shape, dt):
        self.uid += 1
        return Buf(self.pes.enter_context(self.nc.psum_tensor("%s_%d" % (name, self.uid), shape, dt)))

    def gsb(self, name, shape, dt):
        return Buf(self.ges.enter_context(self.nc.sbuf_tensor(name, shape, dt)))

    def begin_phase(self):
        self.pes = ExitStack()
        self.pes.__enter__()

    def end_phase(self):
        self.P.barrier()
        self.P.emit()
        self.pes.close()
        self.phase_no += 1

    def load_w(self, dst, src2d, kc_n, chunk=None):
        P = self.P
        v = src2d.rearrange("(k p) n -> p k n", p=128)
        step = chunk or 1
        for k in range(0, kc_n, step):
            k1 = min(kc_n, k + step)
            P.dma("gpsimd", dst[:, k:k1, :], v[:, k:k1, :], dst, W=[dst])

    def load_bcast(self, dst, vec1d, n):
        self.P.dma("sync", dst[:, 0:n], vec1d.partition_broadcast(128), dst, W=[dst])

    def load_x(self, xt, src, i, nsub=NSUB):
        rows = nsub * 128
        v = src[i * rows:(i + 1) * rows, :].rearrange("(s p) d -> p s d", p=128)
        self.P.dma("sync", xt[:, 0:nsub, :], v, xt, W=[xt])

    def store_x(self, xt, dst, i, nsub=NSUB):
        rows = nsub * 128
        v = dst[i * rows:(i + 1) * rows, :].rearrange("(s p) d -> p s d", p=128)
        self.P.dma("sync", v, xt[:, 0:nsub, :], xt, R=[xt])

    def norm(self, xt, gb, hn, st, nsub=NSUB):
        P = self.P
        ss, rs, junk = st
        for s in range(nsub):
            P.op("scalar", lambda e, s=s: e.activation(out=junk[:, :], in_=xt[:, s, :], func=AF.Square,
                                                      accum_out=ss[:, s:s + 1]),
                 R=[xt], W=[junk, ss])
        P.op("scalar", lambda e: e.activation(out=rs[:, 0:nsub], in_=ss[:, 0:nsub], func=AF.Sqrt,
                                              scale=1.0 / D, bias=EPS), R=[ss], W=[rs])
        P.op("vector", lambda e: e.reciprocal(out=rs[:, 0:nsub], in_=rs[:, 0:nsub]), R=[rs], W=[rs])
        for s in range(nsub):
            P.op("vector", lambda e, s=s: e.scalar_tensor_tensor(out=hn[:, s, :], in0=xt[:, s, :],
                                                                scalar=rs[:, s:s + 1], in1=gb[:, :],
                                                                op0=ALU.mult, op1=ALU.mult),
                 R=[xt, rs, gb], W=[hn])

    def norm_state(self):
        return (self.sb("ss", [128, NSUB], F32), self.sb("rs", [128, NSUB], F32),
                self.sb("junk", [128, D], BF16))

    def transposes(self, hn, hT, ptr, nsub=NSUB):
        P = self.P
        ident = self.ident_b
        w = nsub * 128
        for kc in range(KC):
            pt = ptr[kc % 2]
            for s in range(nsub):
                P.op("tensor", lambda e, pt=pt, s=s, kc=kc: e.transpose(out=pt[:, s * 128:(s + 1) * 128],
                                                                        in_=hn[:, s, kc * 128:(kc + 1) * 128],
                                                                        identity=ident[:, :]),
                     R=[hn, ident], W=[pt], signal=(s == nsub - 1))
            P.op("scalar", lambda e, pt=pt, kc=kc: e.copy(out=hT[:, kc, 0:w], in_=pt[:, 0:w]), R=[pt], W=[hT])

    def mm_group(self, out_ps, out_sl, pairs, R):
        P = self.P
        n = len(pairs)
        ev = None
        for k, (l, r) in enumerate(pairs):
            ev = P.op("tensor", lambda e, l=l, r=r, k=k: e.matmul(out_ps.t[out_sl], lhsT=l, rhs=r,
                                                                 start=(k == 0), stop=(k == n - 1)),
                      R=R, W=[out_ps], signal=(k == n - 1))
        return ev

    def phase_pre(self):
        P = self.P
        self.begin_phase()
        xt = self.sb("xt", [128, NSUB, D], F32)
        gb = self.sb("gb", [128, D], F32)
        hn = self.sb("hn", [128, NSUB, D], BF16)
        st = self.norm_state()
        ptr = [self.ps("ptr0", [128, 1024], BF16), self.ps("ptr1", [128, 1024], BF16)]
        self.load_bcast(gb, self.d["mem_norm"], D)
        self.load_x(xt, self.d["mem"], 0, nsub=2)
        self.norm(xt, gb, hn, st, nsub=2)
        self.transposes(hn, self.memT, ptr, nsub=2)
        self.end_phase()

    def mkv(self, l, ptr_unused, pg):
        P = self.P
        wkv = self.sb("wkv", [128, KC, 1024], BF16)
        self.load_w(wkv, self.d["mem_w_kv"][l], KC, chunk=4)
        mkT = self.sb("mkT", [128, MH, M], BF16)
        mv = self.sb("mv", [128, 2, 512], BF16)
        memT = self.memT
        for m in range(MH):
            pb = pg[m % len(pg)]
            self.mm_group(pb, (slice(None), slice(0, M)),
                          [(wkv[:, kc, m * 128:(m + 1) * 128], memT[:, kc, :]) for kc in range(KC)],
                          R=[wkv, memT])
            P.op("vector", lambda e, pb=pb, m=m: e.tensor_copy(out=mkT[:, m, :], in_=pb[:, 0:M]), R=[pb], W=[mkT])
        for mc in range(2):
            pb = pg[mc % len(pg)]
            self.mm_group(pb, (slice(None), slice(0, 512)),
                          [(memT[:, kc, mc * 128:(mc + 1) * 128], wkv[:, kc, 512:1024]) for kc in range(KC)],
                          R=[wkv, memT])
            P.op("vector", lambda e, pb=pb, mc=mc: e.tensor_copy(out=mv[:, mc, :], in_=pb[:, 0:512]), R=[pb], W=[mv])
        return mkT, mv

    def mem_attn(self, hT, w, col0, mkT, mv, pg, qmT, pT, rden, dst_fn):
        P = self.P
        ones = self.ones_b
        for m in range(MH):
            pq = pg[0]
            self.mm_group(pq, (slice(None), slice(0, T)),
                          [(w[:, kc, col0 + m * 128: col0 + (m + 1) * 128], hT[:, kc, :]) for kc in range(KC)],
                          R=[w, hT])
            P.op("scalar", lambda e, pq=pq: e.copy(out=qmT[:, :], in_=pq[:, :]), R=[pq], W=[qmT])
            for mc in range(2):
                pl = pg[1 + mc]
                self.mm_group(pl, (slice(None), slice(0, T)), [(mkT[:, m, mc * 128:(mc + 1) * 128], qmT[:, :])],
                              R=[mkT, qmT])
                P.op("scalar", lambda e, pl=pl, mc=mc: e.activation(out=pT[mc][:, :], in_=pl[:, :], func=AF.Exp,
                                                                   scale=SCALE), R=[pl], W=[pT[mc]])
            po, pd = pg[3], pg[4]
            self.mm_group(po, (slice(None), slice(0, T)),
                          [(mv[:, mc, m * 128:(m + 1) * 128], pT[mc][:, :]) for mc in range(2)], R=[mv, pT[0], pT[1]])
            self.mm_group(pd, (slice(None), slice(0, T)),
                          [(ones[:, :], pT[mc][:, :]) for mc in range(2)], R=[ones, pT[0], pT[1]])
            P.op("vector", lambda e, pd=pd: e.reciprocal(out=rden[:, :], in_=pd[:, :]), R=[pd], W=[rden])
            db, dap = dst_fn(m)
            P.op("vector", lambda e, po=po, dap=dap: e.tensor_tensor(out=dap, in0=po[:, :], in1=rden[:, :],
                                                                    op=ALU.mult), R=[po, rden], W=[db])

    def phase_ffn(self, l, j, src, dst):
        P = self.P
        self.begin_phase()
        NT = self.NT
        wgu = self.sb("wgu", [128, KC, 2 * DFF], BF16)
        wd = self.sb("wd", [128, FC, D], BF16)
        gb = self.sb("gb", [128, D], F32)
        self.load_bcast(gb, self.d["ffn_norm"][l, j], D)
        self.load_w(wgu, self.d["ffn_w_gate_up"][l, j], KC)
        self.load_w(wd, self.d["ffn_w_down"][l, j], FC, chunk=6)
        xn = [self.sb("xn", [128, D], F32) for _ in range(2)]
        xr = [self.sb("xr", [128, D], F32) for _ in range(2)]
        hn = self.sb("hn", [128, NSUB, D], BF16)
        hT = self.sb("hT", [128, KC, T], BF16)
        actT = [self.sb("actT", [128, T], BF16) for _ in range(FC)]
        sg = [self.sb("sg", [128, T], F32) for _ in range(2)]
        ss = self.sb("ss", [128, NSUB], F32)
        rs = self.sb("rs", [128, NSUB], F32)
        ptr = [self.ps("ptr0", [128, 1024], BF16), self.ps("ptr1", [128, 1024], BF16)]
        pgu = [(self.ps("pg", [128, T], F32), self.ps("pu", [128, T], F32)) for _ in range(2)]
        pdn = [self.ps("pd", [128, T], F32) for _ in range(2)]

        def rows(ap, i, s):
            r0 = i * T + s * 128
            return ap[r0:r0 + 128, :]

        def norm_tile(i):
            for s in range(NSUB):
                xb = xn[s % 2]
                P.dma("sync", xb[:, :], rows(src, i, s), xb, W=[xb])
                P.op("scalar", lambda e, s=s, xb=xb: e.activation(out=hn[:, s, :], in_=xb[:, :], func=AF.Square,
                                                                  accum_out=ss[:, s:s + 1]), R=[xb], W=[hn, ss])
                P.op("scalar", lambda e, s=s: e.activation(out=rs[:, s:s + 1], in_=ss[:, s:s + 1], func=AF.Sqrt,
                                                           scale=1.0 / D, bias=EPS), R=[ss], W=[rs])
                P.op("vector", lambda e, s=s: e.reciprocal(out=rs[:, s:s + 1], in_=rs[:, s:s + 1]), R=[rs], W=[rs])
                P.op("vector", lambda e, s=s, xb=xb: e.scalar_tensor_tensor(out=hn[:, s, :], in0=xb[:, :],
                                                                          scalar=rs[:, s:s + 1], in1=gb[:, :],
                                                                          op0=ALU.mult, op1=ALU.mult),
                     R=[xb, rs, gb], W=[hn])

        def down(i):
            n = 0

            def ld(s):
                P.dma("sync", xr[s % 2][:, :], rows(src, i, s), xr[s % 2], W=[xr[s % 2]])
            ld(0)
            ld(1)
            for s in range(NSUB):
                xb = xr[s % 2]
                for hf in range(2):
                    pb = pdn[n % 2]
                    n += 1
                    self.mm_group(pb, (slice(None), slice(0, 512)),
                                  [(actT[c][:, s * 128:(s + 1) * 128], wd[:, c, hf * 512:(hf + 1) * 512])
                                   for c in range(FC)], R=[wd] + actT)
                    P.op("vector", lambda e, pb=pb, hf=hf, xb=xb: e.scalar_tensor_tensor(
                        out=xb[:, hf * 512:(hf + 1) * 512], in0=pb[:, :], scalar=0.5,
                        in1=xb[:, hf * 512:(hf + 1) * 512], op0=ALU.mult, op1=ALU.add), R=[pb, xb], W=[xb])
                P.dma("sync", rows(dst, i, s), xb[:, :], xb, R=[xb])
                if s + 2 < NSUB:
                    ld(s + 2)

        norm_tile(0)
        self.transposes(hn, hT, ptr)
        for i in range(NT):
            for c in range(FC):
                pg_, pu_ = pgu[c % 2]
                self.mm_group(pg_, (slice(None), slice(0, T)),
                              [(wgu[:, kc, c * 128:(c + 1) * 128], hT[:, kc, :]) for kc in range(KC)], R=[wgu, hT])
                self.mm_group(pu_, (slice(None), slice(0, T)),
                              [(wgu[:, kc, DFF + c * 128: DFF + (c + 1) * 128], hT[:, kc, :]) for kc in range(KC)],
                              R=[wgu, hT])
                sgb = sg[c % 2]
                P.op("scalar", lambda e, pg_=pg_, sgb=sgb: e.activation(out=sgb[:, :], in_=pg_[:, :], func=AF.Silu),
                     R=[pg_], W=[sgb])
                P.op("vector", lambda e, pu_=pu_, sgb=sgb, c=c: e.tensor_tensor(out=actT[c][:, :], in0=pu_[:, :],
                                                                              in1=sgb[:, :], op=ALU.mult),
                     R=[pu_, sgb], W=[actT[c]])
                if c == 6 and i + 1 < NT:
                    norm_tile(i + 1)
            if i + 1 < NT:
                self.transposes(hn, hT, ptr)
            down(i)
        self.end_phase()

    def phase_mix_a(self, l, src, dst):
        P = self.P
        self.begin_phase()
        NT = self.NT
        win = self.sb("win", [128, KC, 3584], BF16)
        wout = self.sb("wout", [128, 12, D], BF16)
        gb = self.sb("gb", [128, D], F32)
        cw = self.sb("cw", [128, 3, KC], F32)
        self.load_bcast(gb, self.d["mix_norm"][l], D)
        for k in range(3):
            P.dma("sync", cw[:, k, :], self.d["a_conv_w"][l, k].rearrange("(c p) -> p c", p=128), cw, W=[cw],
                  noncontig=True)
        self.load_w(win, self.d["a_w_in"][l], KC, chunk=2)
        self.load_w(wout, self.d["a_w_out"][l], 12, chunk=6)
        pg = [self.ps("pg%d" % k, [128, T], F32) for k in range(6)]
        ptr = [self.ps("ptr0", [128, 1024], BF16), self.ps("ptr1", [128, 1024], BF16)]
        mkT, mv = self.mkv(l, ptr, pg)
        xts = [self.sb("xt", [128, NSUB, D], F32) for _ in range(2)]
        hn = self.sb("hn", [128, NSUB, D], BF16)
        hT = self.sb("hT", [128, KC, T], BF16)
        st = self.norm_state()
        yT = [self.sb("yT", [128, T], BF16) for _ in range(12)]
        gcs = [self.sb("gcs", [128, T], F32) for _ in range(2)]
        vb = [self.sb("vb", [128, T + 2], F32) for _ in range(2)]
        acc = [self.sb("acc", [128, T], F32) for _ in range(2)]
        halo = [self.sb("halo", [128, 2], F32) for _ in range(KC)]
        qmT = self.sb("qmT", [128, T], BF16)
        pT = [self.sb("pT", [128, T], BF16) for _ in range(2)]
        rden = self.sb("rden", [128, T], F32)
        for c in range(KC):
            P.op("gpsimd", lambda e, c=c: e.memset(halo[c][:, :], 0.0), W=[halo[c]])

        self.load_x(xts[0], src, 0)
        for i in range(NT):
            xt = xts[i % 2]
            if i + 1 < NT:
                self.load_x(xts[(i + 1) % 2], src, i + 1)
            self.norm(xt, gb, hn, st)
            self.transposes(hn, hT, ptr)
            for c in range(KC):
                pc, pu, pb = pg[(c % 2) * 3], pg[(c % 2) * 3 + 1], pg[(c % 2) * 3 + 2]
                for (pp, col) in ((pc, 1024 + c * 128), (pu, 2048 + c * 128), (pb, c * 128)):
                    self.mm_group(pp, (slice(None), slice(0, T)),
                                  [(win[:, kc, col:col + 128], hT[:, kc, :]) for kc in range(KC)], R=[win, hT])
                g_, v_, a_ = gcs[c % 2], vb[c % 2], acc[c % 2]
                P.op("scalar", lambda e, pc=pc, g_=g_: e.copy(out=g_[:, :], in_=pc[:, :]), R=[pc], W=[g_])
                P.op("gpsimd", lambda e, v_=v_, c=c: e.tensor_copy(out=v_[:, 0:2], in_=halo[c][:, :]),
                     R=[halo[c]], W=[v_])
                P.op("vector", lambda e, pu=pu, g_=g_, v_=v_: e.tensor_tensor(out=v_[:, 2:T + 2], in0=pu[:, :],
                                                                             in1=g_[:, :], op=ALU.mult),
                     R=[pu, g_], W=[v_])
                P.op("gpsimd", lambda e, v_=v_, c=c: e.tensor_copy(out=halo[c][:, :], in_=v_[:, T:T + 2]),
                     R=[v_], W=[halo[c]])
                P.op("gpsimd", lambda e, v_=v_, a_=a_, c=c: e.tensor_scalar_mul(out=a_[:, :], in0=v_[:, 2:T + 2],
                                                                               scalar1=cw[:, 2, c:c + 1]),
                     R=[v_, cw], W=[a_])
                for k in (1, 0):
                    P.op("vector", lambda e, v_=v_, a_=a_, c=c, k=k: e.scalar_tensor_tensor(
                        out=a_[:, :], in0=v_[:, k:k + T], scalar=cw[:, k, c:c + 1], in1=a_[:, :],
                        op0=ALU.mult, op1=ALU.add), R=[v_, cw, a_], W=[a_])
                P.op("vector", lambda e, pb=pb, a_=a_, c=c: e.tensor_tensor(out=yT[c][:, :], in0=pb[:, :],
                                                                           in1=a_[:, :], op=ALU.mult),
                     R=[pb, a_], W=[yT[c]])
            self.mem_attn(hT, win, 3072, mkT, mv, pg, qmT, pT, rden, lambda m: (yT[8 + m], yT[8 + m][:, :]))
            n = 0
            for s in range(NSUB):
                for hf in range(2):
                    pb = pg[n % 2]
                    n += 1
                    self.mm_group(pb, (slice(None), slice(0, 512)),
                                  [(yT[c][:, s * 128:(s + 1) * 128], wout[:, c, hf * 512:(hf + 1) * 512])
                                   for c in range(12)], R=[wout] + yT)
                    P.op("vector", lambda e, pb=pb, s=s, hf=hf, xt=xt: e.tensor_tensor(
                        out=xt[:, s, hf * 512:(hf + 1) * 512], in0=pb[:, :],
                        in1=xt[:, s, hf * 512:(hf + 1) * 512], op=ALU.add), R=[pb, xt], W=[xt])
            self.store_x(xt, dst, i)
        self.end_phase()

    def phase_kvf(self, src):
        P = self.P
        self.begin_phase()
        NT = self.NT
        d = self.d
        w = self.sb("wkvf", [128, KC, 2056], BF16)
        gb = self.sb("gb", [128, D], F32)
        self.load_bcast(gb, d["kv_norm"], D)
        self.load_w(w, d["w_kvf"], KC, chunk=2)
        bfb = self.sb("bfb", [128, 32], F32)
        for s in range(NSUB):
            P.dma("sync", bfb[:, s * 8:(s + 1) * 8], d["b_f"].partition_broadcast(128), bfb, W=[bfb])
        xts = [self.sb("xt", [128, NSUB, D], F32) for _ in range(2)]
        hn = self.sb("hn", [128, NSUB, D], BF16)
        hT = self.sb("hT", [128, KC, T], BF16)
        st = self.norm_state()
        ptr = [self.ps("ptr0", [128, 1024], BF16), self.ps("ptr1", [128, 1024], BF16)]
        pg = [self.ps("pg%d" % k, [128, T], F32) for k in range(3)]
        pf = self.ps("pf", [128, T], F32)
        pc = self.ps("pc", [128, T], F32)
        pct = self.ps("pct", [128, T], F32)
        kst = [self.sb("kst", [128, H, T], BF16) for _ in range(2)]
        vst = [self.sb("vst", [128, NSUB, D], BF16) for _ in range(2)]
        fl = self.sb("fl", [128, 32], F32)
        ex = self.sb("ex", [128, 32], F32)
        sp = self.sb("sp", [128, 32], F32)
        gprev = self.sb("gprev", [128, 8], F32)
        csb = self.sb("csb", [128, 32], F32)
        cst = [self.sb("cst", [8, T], F32) for _ in range(2)]
        ntri, nones, identf = self.ntri_f, self.nones_f, self.ident_f
        negc = self.negc
        P.op("gpsimd", lambda e: e.memset(gprev[:, :], 0.0), W=[gprev])
        kT_v = d["kT"].rearrange("h p s -> p h s")
        vP_v = d["vP"].rearrange("h p c e -> p c h e")

        self.load_x(xts[0], src, 0)
        for i in range(NT):
            xt = xts[i % 2]
            if i + 1 < NT:
                self.load_x(xts[(i + 1) % 2], src, i + 1)
            self.norm(xt, gb, hn, st)
            self.transposes(hn, hT, ptr)
            ks, vs, cs = kst[i % 2], vst[i % 2], cst[i % 2]
            for h in range(H):
                pb = pg[h % 3]
                self.mm_group(pb, (slice(None), slice(0, T)),
                              [(w[:, kc, h * 128:(h + 1) * 128], hT[:, kc, :]) for kc in range(KC)], R=[w, hT])
                if h % 2 == 0:
                    P.op("scalar", lambda e, pb=pb, h=h, ks=ks: e.copy(out=ks[:, h, :], in_=pb[:, :]), R=[pb], W=[ks])
                else:
                    P.op("vector", lambda e, pb=pb, h=h, ks=ks: e.tensor_copy(out=ks[:, h, :], in_=pb[:, :]),
                         R=[pb], W=[ks])
            P.dma("sync", kT_v[:, :, i * T:(i + 1) * T], ks[:, :, :], ks, R=[ks])
            n = 0
            for s in range(NSUB):
                for hf in range(2):
                    pb = pg[n % 3]
                    n += 1
                    self.mm_group(pb, (slice(None), slice(0, 512)),
                                  [(hT[:, kc, s * 128:(s + 1) * 128], w[:, kc, 1024 + hf * 512: 1024 + (hf + 1) * 512])
                                   for kc in range(KC)], R=[w, hT])
                    if n % 2 == 0:
                        P.op("scalar", lambda e, pb=pb, s=s, hf=hf, vs=vs: e.copy(
                            out=vs[:, s, hf * 512:(hf + 1) * 512], in_=pb[:, :]), R=[pb], W=[vs])
                    else:
                        P.op("vector", lambda e, pb=pb, s=s, hf=hf, vs=vs: e.tensor_copy(
                            out=vs[:, s, hf * 512:(hf + 1) * 512], in_=pb[:, :]), R=[pb], W=[vs])
            for s in range(NSUB):
                P.dma("sync", vP_v[:, i * NSUB + s, :, :],
                      vs[:, s, :].rearrange("p (h e) -> p h e", h=H), vs, R=[vs])
            for s in range(NSUB):
                self.mm_group(pf, (slice(None), slice(s * 8, (s + 1) * 8)),
                              [(hT[:, kc, s * 128:(s + 1) * 128], w[:, kc, 2048:2056]) for kc in range(KC)], R=[w, hT])
            P.op("vector", lambda e: e.tensor_tensor(out=fl[:, :], in0=pf[:, 0:32], in1=bfb[:, :], op=ALU.add),
                 R=[pf, bfb], W=[fl])
            P.op("scalar", lambda e: e.activation(out=ex[:, :], in_=fl[:, :], func=AF.Exp, scale=-1.0), R=[fl], W=[ex])
            P.op("scalar", lambda e: e.activation(out=sp[:, :], in_=ex[:, :], func=AF.Ln, bias=1.0), R=[ex], W=[sp])
            for s in range(NSUB):
                self.mm_group(pc, (slice(None), slice(s * 8, (s + 1) * 8)),
                              [(ntri[:, :], sp[:, s * 8:(s + 1) * 8]), (nones[:, :], gprev[:, :])],
                              R=[ntri, nones, sp, gprev])
                P.op("vector", lambda e, s=s: e.tensor_tensor(out=gprev[:, :], in0=gprev[:, :],
                                                              in1=sp[:, s * 8:(s + 1) * 8], op=ALU.add),
                     R=[sp, gprev], W=[gprev])
            P.op("vector", lambda e: e.tensor_copy(out=csb[:, :], in_=pc[:, 0:32]), R=[pc], W=[csb])
            P.op("gpsimd", lambda e, i=i: e.tensor_scalar_mul(out=negc[:, i * NSUB:(i + 1) * NSUB, :],
                                                              in0=csb[:, :].rearrange("p (s h) -> p s h", h=H),
                                                              scalar1=-1.0), R=[csb], W=[negc])
            for s in range(NSUB):
                P.op("tensor", lambda e, s=s: e.transpose(out=pct[0:8, s * 128:(s + 1) * 128],
                                                          in_=csb[:, s * 8:(s + 1) * 8], identity=identf[:, :]),
                     R=[csb, identf], W=[pct], signal=(s == NSUB - 1))
            P.op("vector", lambda e, cs=cs: e.tensor_copy(out=cs[:, :], in_=pct[0:8, :]), R=[pct], W=[cs])
            P.dma("sync", d["cT"][:, i * T:(i + 1) * T], cs[:, :], cs, R=[cs])
        self.end_phase()

    def phase_b1(self, l, src):
        P = self.P
        self.begin_phase()
        NT = self.NT
        d = self.d
        jl = l - N_A
        wq = self.sb("wq", [128, KC, 1536], BF16)
        gb = self.sb("gb", [128, D], F32)
        self.load_bcast(gb, d["mix_norm"][l], D)
        self.load_w(wq, d["b_w_q"][jl], KC, chunk=4)
        pg = [self.ps("pg%d" % k, [128, T], F32) for k in range(6)]
        ptr = [self.ps("ptr0", [128, 1024], BF16), self.ps("ptr1", [128, 1024], BF16)]
        mkT, mv = self.mkv(l, ptr, pg)
        xts = [self.sb("xt", [128, NSUB, D], F32) for _ in range(2)]
        hn = self.sb("hn", [128, NSUB, D], BF16)
        hT = self.sb("hT", [128, KC, T], BF16)
        st = self.norm_state()
        qst = [self.sb("qst", [128, H, T], BF16) for _ in range(2)]
        yms = [self.sb("yms", [128, MH, T], BF16) for _ in range(2)]
        qmT = self.sb("qmT", [128, T], BF16)
        pT = [self.sb("pT", [128, T], BF16) for _ in range(2)]
        rden = self.sb("rden", [128, T], F32)
        qT_v = d["qT"].rearrange("h p s -> p h s")
        yT_v = d["yT"].rearrange("c p s -> p c s")
        self.load_x(xts[0], src, 0)
        for i in range(NT):
            xt = xts[i % 2]
            if i + 1 < NT:
                self.load_x(xts[(i + 1) % 2], src, i + 1)
            self.norm(xt, gb, hn, st)
            self.transposes(hn, hT, ptr)
            qs, ym = qst[i % 2], yms[i % 2]
            for h in range(H):
                pb = pg[h % 2]
                self.mm_group(pb, (slice(None), slice(0, T)),
                              [(wq[:, kc, h * 128:(h + 1) * 128], hT[:, kc, :]) for kc in range(KC)], R=[wq, hT])
                if h % 2 == 0:
                    P.op("scalar", lambda e, pb=pb, h=h, qs=qs: e.copy(out=qs[:, h, :], in_=pb[:, :]), R=[pb], W=[qs])
                else:
                    P.op("vector", lambda e, pb=pb, h=h, qs=qs: e.tensor_copy(out=qs[:, h, :], in_=pb[:, :]),
                         R=[pb], W=[qs])
            P.dma("sync", qT_v[:, :, i * T:(i + 1) * T], qs[:, :, :], qs, R=[qs])
            self.mem_attn(hT, wq, 1024, mkT, mv, pg, qmT, pT, rden, lambda m, ym=ym: (ym, ym[:, m, :]))
            P.dma("sync", yT_v[:, 8:12, i * T:(i + 1) * T], ym[:, :, :], ym, R=[ym])
        self.end_phase()

    def phase_b2(self):
        P = self.P
        self.begin_phase()
        NT, S, NCH = self.NT, self.S, self.NCH
        d = self.d
        LA = 3
        NPL, NTMP, NPT = 4, 4, 6
        kTh = [self.sb("kTh", [128, S], BF16) for _ in range(2)]
        vh = [self.sb("vh", [128, NCH, 128], BF16) for _ in range(2)]
        qTt = [self.sb("qTt", [128, T], BF16) for _ in range(2)]
        cb = [self.sb("cb", [128, T], F32) for _ in range(2)]
        cbm = [[self.sb("cbm", [128, T], F32) for _ in range(4)] for _ in range(2)]
        tmp = [self.sb("tmp", [128, T], F32) for _ in range(NTMP)]
        pT = [self.sb("pT", [128, T], BF16) for _ in range(NPT)]
        rden = self.sb("rden", [128, T], F32)
        yst = [self.sb("yst", [128, T], BF16) for _ in range(2)]
        pl = [self.ps("pl%d" % k, [128, T], F32) for k in range(NPL)]
        po = [self.ps("po%d" % k, [128, T], F32) for k in range(2)]
        pd = [self.ps("pd%d" % k, [128, T], F32) for k in range(2)]
        ones, negc = self.ones_b, self.negc
        maskn = self.sb("maskn", [128, 4, T], F32)
        P.dma("sync", maskn[:, :, :], d["c_mask"], maskn, W=[maskn])

        def load_head(h):
            P.dma("sync", kTh[h % 2][:, :], d["kT"][h], kTh[h % 2], W=[kTh[h % 2]])
            P.dma("sync", vh[h % 2][:, :, :], d["vP"][h], vh[h % 2], W=[vh[h % 2]])

        def load_qc(it):
            h, i = divmod(it, NT)
            q_, c_ = qTt[it % 2], cb[it % 2]
            P.dma("sync", q_[:, :], d["qT"][h, :, i * T:(i + 1) * T], q_, W=[q_])
            P.dma("sync", c_[:, :], d["cT"][h, i * T:(i + 1) * T].partition_broadcast(128), c_, W=[c_])

        blocks = []
        for it in range(H * NT):
            h, i = divmod(it, NT)
            nj = 4 * i + 4
            for j in range(nj):
                blocks.append((it, h, i, j, nj))
        NB = len(blocks)

        def stage1(n):
            it, h, i, j, nj = blocks[n]
            kb = kTh[h % 2]
            q_, c_, cm_ = qTt[it % 2], cb[it % 2], cbm[it % 2]
            if j == 0:
                for jj in range(4):
                    P.op("gpsimd", lambda e, jj=jj, c_=c_, cm_=cm_: e.tensor_tensor(
                        out=cm_[jj][:, :], in0=c_[:, :], in1=maskn[:, jj, :], op=ALU.add),
                        R=[c_, maskn], W=[cm_[jj]])
                if it + 1 < H * NT:
                    load_qc(it + 1)
                if h + 1 < H and ((NT > 1 and i == 1) or (NT == 1 and i == 0)):
                    load_head(h + 1)
            jj = j - 4 * i
            bias_t = c_ if jj < 0 else cm_[jj]
            l_, t_, p_ = pl[n % NPL], tmp[n % NTMP], pT[n % NPT]
            self.mm_group(l_, (slice(None), slice(0, T)), [(kb[:, j * 128:(j + 1) * 128], q_[:, :])], R=[kb, q_])
            P.op("vector", lambda e, l_=l_, t_=t_, bias_t=bias_t: e.scalar_tensor_tensor(
                out=t_[:, :], in0=l_[:, :], scalar=SCALE, in1=bias_t[:, :], op0=ALU.mult, op1=ALU.add),
                R=[l_, bias_t], W=[t_])
            P.op("scalar", lambda e, t_=t_, p_=p_, j=j, h=h: e.activation(
                out=p_[:, :], in_=t_[:, :], func=AF.Exp, bias=negc[:, j, h:h + 1]), R=[t_, negc], W=[p_])

        def stage2(n):
            it, h, i, j, nj = blocks[n]
            vb_ = vh[h % 2]
            o_, d_, y_ = po[it % 2], pd[it % 2], yst[it % 2]
            p_ = pT[n % NPT]
            P.op("tensor", lambda e, o_=o_, p_=p_, j=j, vb_=vb_, nj=nj: e.matmul(
                o_[:, :], lhsT=vb_[:, j, :], rhs=p_[:, :], start=(j == 0), stop=(j == nj - 1)),
                R=[vb_, p_], W=[o_], signal=False)
            P.op("tensor", lambda e, d_=d_, p_=p_, j=j, nj=nj: e.matmul(
                d_[:, :], lhsT=ones[:, :], rhs=p_[:, :], start=(j == 0), stop=(j == nj - 1)),
                R=[ones, p_], W=[d_], signal=True)
            if j == nj - 1:
                P.op("vector", lambda e, d_=d_: e.reciprocal(out=rden[:, :], in_=d_[:, :]), R=[d_], W=[rden])
                P.op("vector", lambda e, o_=o_, y_=y_: e.tensor_tensor(out=y_[:, :], in0=o_[:, :], in1=rden[:, :],
                                                                      op=ALU.mult), R=[o_, rden], W=[y_])
                P.dma("sync", d["yT"][h, :, i * T:(i + 1) * T], y_[:, :], y_, R=[y_])

        load_head(0)
        load_qc(0)
        if NT == 1 and H > 1:
            pass
        for n in range(NB + LA):
            if n < NB:
                stage1(n)
            if n >= LA:
                stage2(n - LA)
        self.end_phase()

    def phase_b3(self, l, src, dst):
        P = self.P
        self.begin_phase()
        NT = self.NT
        d = self.d
        jl = l - N_A
        wout = self.sb("wout", [128, 12, D], BF16)
        self.load_w(wout, d["b_w_out"][jl], 12, chunk=6)
        xts = [self.sb("xt", [128, NSUB, D], F32) for _ in range(2)]
        yTt = [self.sb("yTt", [128, 12, T], BF16) for _ in range(2)]
        pg = [self.ps("pg%d" % k, [128, T], F32) for k in range(2)]
        yT_v = d["yT"].rearrange("c p s -> p c s")

        def loads(i):
            self.load_x(xts[i % 2], src, i)
            P.dma("sync", yTt[i % 2][:, :, :], yT_v[:, :, i * T:(i + 1) * T], yTt[i % 2], W=[yTt[i % 2]])
        loads(0)
        for i in range(NT):
            xt, yt = xts[i % 2], yTt[i % 2]
            if i + 1 < NT:
                loads(i + 1)
            n = 0
            for s in range(NSUB):
                for hf in range(2):
                    pb = pg[n % 2]
                    n += 1
                    self.mm_group(pb, (slice(None), slice(0, 512)),
                                  [(yt[:, c, s * 128:(s + 1) * 128], wout[:, c, hf * 512:(hf + 1) * 512])
                                   for c in range(12)], R=[wout, yt])
                    P.op("vector", lambda e, pb=pb, s=s, hf=hf, xt=xt: e.tensor_tensor(
                        out=xt[:, s, hf * 512:(hf + 1) * 512], in0=pb[:, :],
                        in1=xt[:, s, hf * 512:(hf + 1) * 512], op=ALU.add), R=[pb, xt], W=[xt])
            self.store_x(xt, dst, i)
        self.end_phase()

    def phase_final(self, src, dst):
        P = self.P
        self.begin_phase()
        NT = self.NT
        gb = self.sb("gb", [128, D], F32)
        self.load_bcast(gb, self.d["final_norm"], D)
        xts = [self.sb("xt", [128, NSUB, D], F32) for _ in range(2)]
        ots = [self.sb("ot", [128, NSUB, D], F32) for _ in range(2)]
        st = self.norm_state()
        self.load_x(xts[0], src, 0)
        for i in range(NT):
            if i + 1 < NT:
                self.load_x(xts[(i + 1) % 2], src, i + 1)
            self.norm(xts[i % 2], gb, ots[i % 2], st)
            self.store_x(ots[i % 2], dst, i)
        self.end_phase()

    def phase_copy(self, src, dst):
        self.begin_phase()
        xts = [self.sb("xt", [128, NSUB, D], F32) for _ in range(2)]
        for i in range(self.NT):
            self.load_x(xts[i % 2], src, i)
            self.store_x(xts[i % 2], dst, i)
        self.end_phase()

    def build(self):
        nc = self.nc
        S = self.S
        self.uid = 0

        def din(name, shape):
            return nc.dram_tensor(name, list(shape), F32, kind="ExternalInput").ap()
        d = {}
        d["x"] = din("x", [S, D])
        d["mem"] = din("mem", [M, D])
        d["ffn_norm"] = din("ffn_norm", [DEPTH, 2, D])
        d["ffn_w_gate_up"] = din("ffn_w_gate_up", [DEPTH, 2, D, 2 * DFF])
        d["ffn_w_down"] = din("ffn_w_down", [DEPTH, 2, DFF, D])
        d["mix_norm"] = din("mix_norm", [DEPTH, D])
        d["mem_norm"] = din("mem_norm", [D])
        d["mem_w_kv"] = din("mem_w_kv", [DEPTH, D, 1024])
        d["a_w_in"] = din("a_w_in", [N_A, D, 3584])
        d["a_conv_w"] = din("a_conv_w", [N_A, 3, D])
        d["a_w_out"] = din("a_w_out", [N_A, 1536, D])
        d["kv_norm"] = din("kv_norm", [D])
        d["w_kvf"] = din("w_kvf", [D, 2056])
        d["b_f"] = din("b_f", [H])
        d["b_w_q"] = din("b_w_q", [DEPTH - N_A, D, 1536])
        d["b_w_out"] = din("b_w_out", [DEPTH - N_A, 1536, D])
        d["final_norm"] = din("final_norm", [D])
        d["c_ident"] = din("c_ident", [128, 128])
        d["c_ntri"] = din("c_ntri", [128, 128])
        d["c_mask"] = din("c_mask", [128, 4, T])
        d["out"] = nc.dram_tensor("out", [S, D], F32, kind="ExternalOutput").ap()
        d["xs"] = nc.dram_tensor("xs", [S, D], F32).ap()
        d["kT"] = nc.dram_tensor("kT", [H, 128, S], BF16).ap()
        d["vP"] = nc.dram_tensor("vP", [H, 128, S // 128, 128], BF16).ap()
        d["cT"] = nc.dram_tensor("cT", [H, S], F32).ap()
        d["qT"] = nc.dram_tensor("qT", [H, 128, S], BF16).ap()
        d["yT"] = nc.dram_tensor("yT", [12, 128, S], BF16).ap()
        self.d = d

        with ExitStack() as ges:
            self.ges = ges
            self.P = P = Prog(nc, ges)
            self.ident_f = self.gsb("ident_f", [128, 128], F32)
            self.ident_b = self.gsb("ident_b", [128, 128], BF16)
            self.ntri_f = self.gsb("ntri_f", [128, 128], F32)
            self.nones_f = self.gsb("nones_f", [128, 128], F32)
            self.ones_b = self.gsb("ones_b", [128, 128], BF16)
            self.memT = self.gsb("memT", [128, KC, M], BF16)
            self.negc = self.gsb("negc", [128, S // 128, H], F32)
            self.begin_phase()
            P.dma("sync", self.ident_f[:, :], d["c_ident"], self.ident_f, W=[self.ident_f])
            P.dma("gpsimd", self.ident_b[:, :], d["c_ident"], self.ident_b, W=[self.ident_b])
            P.dma("sync", self.ntri_f[:, :], d["c_ntri"], self.ntri_f, W=[self.ntri_f])
            P.op("gpsimd", lambda e: e.memset(self.nones_f[:, :], -1.0), W=[self.nones_f])
            P.op("gpsimd", lambda e: e.memset(self.ones_b[:, :], 1.0), W=[self.ones_b])
            self.end_phase()

            phases = []
            phases.append(lambda: self.phase_pre())
            for l in range(DEPTH):
                src0 = d["x"] if l == 0 else d["xs"]
                if l == N_A:
                    phases.append(lambda: self.phase_kvf(d["xs"]))
                phases.append(lambda l=l, src0=src0: self.phase_ffn(l, 0, src0, d["xs"]))
                if l < N_A:
                    phases.append(lambda l=l: self.phase_mix_a(l, d["xs"], d["xs"]))
                else:
                    phases.append(lambda l=l: self.phase_b1(l, d["xs"]))
                    phases.append(lambda: self.phase_b2())
                    phases.append(lambda l=l: self.phase_b3(l, d["xs"], d["xs"]))
                phases.append(lambda l=l: self.phase_ffn(l, 1, d["xs"], d["xs"]))
            phases.append(lambda: self.phase_final(d["xs"], d["out"]))
            if self.stop_after is not None:
                phases = phases[:self.stop_after]
                phases.append(lambda: self.phase_copy(d["xs"], d["out"]))
            for ph in phases:
                ph()
        return nc


def _consts():
    ident = np.eye(128, dtype=np.float32)
    s = np.arange(128)[:, None]
    t = np.arange(128)[None, :]
    ntri = np.where(s <= t, -1.0, 0.0).astype(np.float32)
    tt = np.arange(T)[None, None, :]
    jj = np.arange(4)[None, :, None]
    mask = np.where(jj * 128 + s[:, :, None] <= tt, 0.0, MASKV).astype(np.float32)
    return {"c_ident": ident, "c_ntri": ntri, "c_mask": np.ascontiguousarray(mask)}


_NC_CACHE = {}


def run(inputs, S, n_cores, stop_after=None):
    key = (S, stop_after)
    if key not in _NC_CACHE:
        _NC_CACHE[key] = Builder(S, stop_after).build()
    nc = _NC_CACHE[key]
    consts = _consts()
    shared = {k: np.ascontiguousarray(np.asarray(v, dtype=np.float32)) for k, v in inputs.items()
              if k not in ("x", "mem")}
    shared.update(consts)
    x = np.asarray(inputs["x"], dtype=np.float32)
    mem = np.asarray(inputs["mem"], dtype=np.float32)
    in_maps = []
    for b in range(n_cores):
        m = dict(shared)
        m["x"] = np.ascontiguousarray(x[b])
        m["mem"] = np.ascontiguousarray(mem[b])
        in_maps.append(m)
    res = run_bass_kernel_spmd(nc, in_maps, core_ids=list(range(n_cores)))
    return np.stack([res.results[b]["out"] for b in range(n_cores)], axis=0)


def kernel(**inputs):
    x = inputs["x"]
    B, S, _ = x.shape
    return run(inputs, S, B).astype(np.float32)
```

```python
import numpy as np
from contextlib import ExitStack
import concourse.bass as bass
import concourse.mybir as mybir
from concourse.bass_utils import run_bass_kernel_spmd

F32 = mybir.dt.float32
BF16 = mybir.dt.bfloat16
AF = mybir.ActivationFunctionType
ALU = mybir.AluOpType

D = 1024
KC = 8
DFF = 2816
FC = 22
T = 512
NSUB = 4
H = 8
MH = 4
M = 256
DEPTH = 4
N_A = 2
EPS = 1e-6
SCALE = 128 ** -0.5
MASKV = -30000.0
ENGS = ("sync", "scalar", "vector", "gpsimd", "tensor")


class Ev:
    __slots__ = ("sem", "val")

    def __init__(self, sem, val):
        self.sem = sem
        self.val = val


class Buf:
    def __init__(self, t, slot=None):
        self.t = t
        self.w = None
        self.r = []
        self.slot = slot

    def __getitem__(self, k):
        return self.t[k]


class Prog:
    def __init__(self, nc, es):
        self.nc = nc
        self.es = es
        self.sem = {e: es.enter_context(nc.semaphore("c_" + e)) for e in ENGS}
        self.cnt = {e: 0 for e in ENGS}
        self.ops = {e: [] for e in ENGS}
        self.waited = {e: {} for e in ENGS}
        self.lazy = {e: [] for e in ENGS}
        self.last = {e: None for e in ENGS}
        self.dma_evs = []
        self.free_slots = {e: [] for e in ENGS}
        self.used_slots = {e: [] for e in ENGS}
        self.nslots = 0
        self.stats = {}

    def slot(self, eng):
        if self.free_slots[eng]:
            s = self.free_slots[eng].pop()
        else:
            self.nslots += 1
            s = [self.es.enter_context(self.nc.semaphore("d%s%d" % (eng[0], self.nslots))), 0]
        self.used_slots[eng].append(s)
        return s

    def _deps(self, R, W, deps, eng=None):
        dl = [d for d in deps if d is not None]
        own = self.sem.get(eng)
        for b in R:
            if b.w is not None:
                dl.append(b.w)
        for b in W:
            if b.w is not None and b.w.sem is not own:
                dl.append(b.w)
            dl.extend(r for r in b.r if r.sem is not own)
        return dl

    def op(self, eng, fn, R=(), W=(), deps=(), signal=True):
        dl = self._deps(R, W, deps, eng)
        ev = Ev(self.sem[eng], None)
        if signal:
            self.cnt[eng] += 1
            ev.val = self.cnt[eng]
            for l in self.lazy[eng]:
                l.val = ev.val
            self.lazy[eng] = []
            self.last[eng] = ev
        else:
            self.lazy[eng].append(ev)
        for b in R:
            b.r.append(ev)
        for b in W:
            b.w = ev
            b.r = []
        self.ops[eng].append((dl, fn, ev if signal else None))
        return ev

    def dma(self, eng, out, in_, slotbuf, R=(), W=(), deps=(), noncontig=False):
        dl = self._deps(R, W, deps)
        if slotbuf.slot is None:
            slotbuf.slot = {}
        if eng not in slotbuf.slot:
            slotbuf.slot[eng] = self.slot(eng)
        sl = slotbuf.slot[eng]
        sl[1] += 16
        ev = Ev(sl[0], sl[1])
        for b in R:
            b.r.append(ev)
        for b in W:
            b.w = ev
            b.r = []
        sem = sl[0]

        nc = self.nc

        def fn(e, out=out, in_=in_, sem=sem):
            if noncontig:
                with nc.allow_non_contiguous_dma(reason="tiny strided gather"):
                    e.dma_start(out=out, in_=in_).then_inc(sem, 16)
            else:
                e.dma_start(out=out, in_=in_).then_inc(sem, 16)
        self.ops[eng].append((dl, fn, None))
        self.dma_evs.append(ev)
        return ev

    def barrier(self):
        for e in ENGS:
            assert not self.lazy[e], "unsignaled tail on " + e
        best = {}
        for ev in [self.last[e] for e in ENGS if self.last[e] is not None] + self.dma_evs:
            k = id(ev.sem)
            if k not in best or best[k].val < ev.val:
                best[k] = ev
        evs = list(best.values())
        for e in ENGS:
            self.ops[e].append((evs, None, None))
        self.dma_evs = []
        for e in ENGS:
            self.free_slots[e].extend(self.used_slots[e])
            self.used_slots[e] = []

    def _run(self, eng, e):
        own = self.sem[eng]
        wd = self.waited[eng]
        st = self.stats.setdefault(eng, [0, 0])
        for (dl, fn, ev) in self.ops[eng]:
            for d in dl:
                if eng == "tensor" and d.sem is own:
                    continue
                assert d.val is not None
                k = id(d.sem)
                if wd.get(k, 0) >= d.val:
                    continue
                wd[k] = d.val
                e.wait_ge(d.sem, d.val)
                st[1] += 1
            if fn is not None:
                st[0] += 1
                ins = fn(e)
                if ev is not None:
                    ins.then_inc(ev.sem, 1)
        self.ops[eng] = []

    def emit(self):
        with self.nc.Block() as block:
            @block.sync
            def _(e):
                self._run("sync", e)

            @block.scalar
            def _(e):
                self._run("scalar", e)

            @block.vector
            def _(e):
                self._run("vector", e)

            @block.gpsimd
            def _(e):
                self._run("gpsimd", e)

            @block.tensor
            def _(e):
                self._run("tensor", e)


class Builder:
    def __init__(self, S, stop_after=None):
        self.S = S
        self.NT = S // T
        self.NCH = S // 128
        self.stop_after = stop_after
        self.nc = bass.Bass("TRN2", target_bir_lowering=False)
        self.phase_no = 0

    def sb(self, name, shape, dt):
        self.uid += 1
        return Buf(self.pes.enter_context(self.nc.sbuf_tensor("%s_%d" % (name, self.uid), shape, dt)))

    def ps(self, name, shape, dt):
        self.uid += 1
        return Buf(self.pes.enter_context(self.nc.psum_tensor("%s_%d" % (name, self.uid), shape, dt)))

    def gsb(self, name, shape, dt):
        return Buf(self.ges.enter_context(self.nc.sbuf_tensor(name, shape, dt)))

    def begin_phase(self):
        self.pes = ExitStack()
        self.pes.__enter__()

    def end_phase(self):
        self.P.barrier()
        self.P.emit()
        self.pes.close()
        self.phase_no += 1

    def load_w(self, dst, src2d, kc_n, chunk=None):
        P = self.P
        v = src2d.rearrange("(k p) n -> p k n", p=128)
        step = chunk or 1
        for k in range(0, kc_n, step):
            k1 = min(kc_n, k + step)
            P.dma("gpsimd", dst[:, k:k1, :], v[:, k:k1, :], dst, W=[dst])

    def load_bcast(self, dst, vec1d, n):
        self.P.dma("sync", dst[:, 0:n], vec1d.partition_broadcast(128), dst, W=[dst])

    def load_x(self, xt, src, i, nsub=NSUB):
        rows = nsub * 128
        v = src[i * rows:(i + 1) * rows, :].rearrange("(s p) d -> p s d", p=128)
        self.P.dma("sync", xt[:, 0:nsub, :], v, xt, W=[xt])

    def store_x(self, xt, dst, i, nsub=NSUB):
        rows = nsub * 128
        v = dst[i * rows:(i + 1) * rows, :].rearrange("(s p) d -> p s d", p=128)
        self.P.dma("sync", v, xt[:, 0:nsub, :], xt, R=[xt])

    def norm(self, xt, gb, hn, st, nsub=NSUB):
        P = self.P
        ss, rs, junk = st
        for s in range(nsub):
            P.op("scalar", lambda e, s=s: e.activation(out=junk[:, :], in_=xt[:, s, :], func=AF.Square,
                                                      accum_out=ss[:, s:s + 1]),
                 R=[xt], W=[junk, ss])
        P.op("scalar", lambda e: e.activation(out=rs[:, 0:nsub], in_=ss[:, 0:nsub], func=AF.Sqrt,
                                              scale=1.0 / D, bias=EPS), R=[ss], W=[rs])
        P.op("vector", lambda e: e.reciprocal(out=rs[:, 0:nsub], in_=rs[:, 0:nsub]), R=[rs], W=[rs])
        for s in range(nsub):
            P.op("vector", lambda e, s=s: e.scalar_tensor_tensor(out=hn[:, s, :], in0=xt[:, s, :],
                                                                scalar=rs[:, s:s + 1], in1=gb[:, :],
                                                                op0=ALU.mult, op1=ALU.mult),
                 R=[xt, rs, gb], W=[hn])

    def norm_state(self):
        return (self.sb("ss", [128, NSUB], F32), self.sb("rs", [128, NSUB], F32),
                self.sb("junk", [128, D], BF16))

    def transposes(self, hn, hT, ptr, nsub=NSUB):
        P = self.P
        ident = self.ident_b
        w = nsub * 128
        for kc in range(KC):
            pt = ptr[kc % 2]
            for s in range(nsub):
                P.op("tensor", lambda e, pt=pt, s=s, kc=kc: e.transpose(out=pt[:, s * 128:(s + 1) * 128],
                                                                        in_=hn[:, s, kc * 128:(kc + 1) * 128],
                                                                        identity=ident[:, :]),
                     R=[hn, ident], W=[pt], signal=(s == nsub - 1))
            P.op("scalar", lambda e, pt=pt, kc=kc: e.copy(out=hT[:, kc, 0:w], in_=pt[:, 0:w]), R=[pt], W=[hT])

    def mm_group(self, out_ps, out_sl, pairs, R):
        P = self.P
        n = len(pairs)
        ev = None
        for k, (l, r) in enumerate(pairs):
            ev = P.op("tensor", lambda e, l=l, r=r, k=k: e.matmul(out_ps.t[out_sl], lhsT=l, rhs=r,
                                                                 start=(k == 0), stop=(k == n - 1)),
                      R=R, W=[out_ps], signal=(k == n - 1))
        return ev

    def phase_pre(self):
        P = self.P
        self.begin_phase()
        xt = self.sb("xt", [128, NSUB, D], F32)
        gb = self.sb("gb", [128, D], F32)
        hn = self.sb("hn", [128, NSUB, D], BF16)
        st = self.norm_state()
        ptr = [self.ps("ptr0", [128, 1024], BF16), self.ps("ptr1", [128, 1024], BF16)]
        self.load_bcast(gb, self.d["mem_norm"], D)
        self.load_x(xt, self.d["mem"], 0, nsub=2)
        self.norm(xt, gb, hn, st, nsub=2)
        self.transposes(hn, self.memT, ptr, nsub=2)
        self.end_phase()

    def mkv(self, l, ptr_unused, pg):
        P = self.P
        wkv = self.sb("wkv", [128, KC, 1024], BF16)
        self.load_w(wkv, self.d["mem_w_kv"][l], KC, chunk=4)
        mkT = self.sb("mkT", [128, MH, M], BF16)
        mv = self.sb("mv", [128, 2, 512], BF16)
        memT = self.memT
        for m in range(MH):
            pb = pg[m % len(pg)]
            self.mm_group(pb, (slice(None), slice(0, M)),
                          [(wkv[:, kc, m * 128:(m + 1) * 128], memT[:, kc, :]) for kc in range(KC)],
                          R=[wkv, memT])
            P.op("vector", lambda e, pb=pb, m=m: e.tensor_copy(out=mkT[:, m, :], in_=pb[:, 0:M]), R=[pb], W=[mkT])
        for mc in range(2):
            pb = pg[mc % len(pg)]
            self.mm_group(pb, (slice(None), slice(0, 512)),
                          [(memT[:, kc, mc * 128:(mc + 1) * 128], wkv[:, kc, 512:1024]) for kc in range(KC)],
                          R=[wkv, memT])
            P.op("vector", lambda e, pb=pb, mc=mc: e.tensor_copy(out=mv[:, mc, :], in_=pb[:, 0:512]), R=[pb], W=[mv])
        return mkT, mv

    def mem_attn(self, hT, w, col0, mkT, mv, pg, qmT, pT, rden, dst_fn, mid=None):
        P = self.P
        ones = self.ones_b
        for m in range(MH):
            pq = pg[m % 2]
            self.mm_group(pq, (slice(None), slice(0, T)),
                          [(w[:, kc, col0 + m * 128: col0 + (m + 1) * 128], hT[:, kc, :]) for kc in range(KC)],
                          R=[w, hT])
            P.op("scalar", lambda e, pq=pq, m=m: e.copy(out=qmT[m][:, :], in_=pq[:, :]), R=[pq], W=[qmT[m]])
        if mid is not None:
            mid()
        for m in range(MH):
            for mc in range(2):
                k = 2 * m + mc
                pl = pg[2 + k % 2]
                self.mm_group(pl, (slice(None), slice(0, T)), [(mkT[:, m, mc * 128:(mc + 1) * 128], qmT[m][:, :])],
                              R=[mkT, qmT[m]])
                P.op("scalar", lambda e, pl=pl, k=k: e.activation(out=pT[k][:, :], in_=pl[:, :], func=AF.Exp,
                                                                 scale=SCALE), R=[pl], W=[pT[k]])
        for m in range(MH):
            po, pd = (pg[4], pg[5]) if m % 2 == 0 else (pg[0], pg[1])
            rd = rden[m % 2]
            self.mm_group(po, (slice(None), slice(0, T)),
                          [(mv[:, mc, m * 128:(m + 1) * 128], pT[2 * m + mc][:, :]) for mc in range(2)],
                          R=[mv, pT[2 * m], pT[2 * m + 1]])
            self.mm_group(pd, (slice(None), slice(0, T)),
                          [(ones[:, :], pT[2 * m + mc][:, :]) for mc in range(2)], R=[ones, pT[2 * m], pT[2 * m + 1]])
            P.op("vector", lambda e, pd=pd, rd=rd: e.reciprocal(out=rd[:, :], in_=pd[:, :]), R=[pd], W=[rd])
            db, dap = dst_fn(m)
            P.op("vector", lambda e, po=po, dap=dap, rd=rd: e.tensor_tensor(out=dap, in0=po[:, :], in1=rd[:, :],
                                                                           op=ALU.mult), R=[po, rd], W=[db])

    def phase_ffn(self, l, j, src, dst):
        P = self.P
        self.begin_phase()
        NT = self.NT
        wgu = self.sb("wgu", [128, KC, 2 * DFF], BF16)
        wd = self.sb("wd", [128, FC, D], BF16)
        gb = self.sb("gb", [128, D], F32)
        self.load_bcast(gb, self.d["ffn_norm"][l, j], D)
        vgu = self.d["ffn_w_gate_up"][l, j].rearrange("(k p) n -> p k n", p=128)
        gblk, ublk = [], []
        for b0 in range(0, DFF, 512):
            b1 = min(DFF, b0 + 512)
            for (lst, off) in ((gblk, 0), (ublk, DFF)):
                bb = Buf(wgu.t)
                P.dma("gpsimd", wgu[:, :, off + b0:off + b1], vgu[:, :, off + b0:off + b1], bb, W=[bb])
                lst.append(bb)
        vdn = self.d["ffn_w_down"][l, j].rearrange("(k p) n -> p k n", p=128)
        wdb = []
        for k0 in range(0, FC, 6):
            k1 = min(FC, k0 + 6)
            bb = Buf(wd.t)
            P.dma("gpsimd", wd[:, k0:k1, :], vdn[:, k0:k1, :], bb, W=[bb])
            wdb.append(bb)
        xn = [self.sb("xn", [128, D], F32) for _ in range(2)]
        xr = [self.sb("xr", [128, D], F32) for _ in range(2)]
        hn = self.sb("hn", [128, NSUB, D], BF16)
        hT = self.sb("hT", [128, KC, T], BF16)
        actT = [self.sb("actT", [128, T], BF16) for _ in range(FC)]
        sg = [self.sb("sg", [128, T], F32) for _ in range(2)]
        ss = self.sb("ss", [128, NSUB], F32)
        rs = self.sb("rs", [128, NSUB], F32)
        ptr = [self.ps("ptr0", [128, 1024], BF16), self.ps("ptr1", [128, 1024], BF16)]
        pgu = [(self.ps("pg", [128, T], F32), self.ps("pu", [128, T], F32)) for _ in range(2)]
        pdn = [self.ps("pd", [128, T], F32) for _ in range(2)]

        def rows(ap, i, s):
            r0 = i * T + s * 128
            return ap[r0:r0 + 128, :]

        def norm_tile(i):
            for s in range(NSUB):
                xb = xn[s % 2]
                P.dma("sync", xb[:, :], rows(src, i, s), xb, W=[xb])
                P.op("scalar", lambda e, s=s, xb=xb: e.activation(out=hn[:, s, :], in_=xb[:, :], func=AF.Square,
                                                                  accum_out=ss[:, s:s + 1]), R=[xb], W=[hn, ss])
                P.op("scalar", lambda e, s=s: e.activation(out=rs[:, s:s + 1], in_=ss[:, s:s + 1], func=AF.Sqrt,
                                                           scale=1.0 / D, bias=EPS), R=[ss], W=[rs])
                P.op("vector", lambda e, s=s: e.reciprocal(out=rs[:, s:s + 1], in_=rs[:, s:s + 1]), R=[rs], W=[rs])
                P.op("vector", lambda e, s=s, xb=xb: e.scalar_tensor_tensor(out=hn[:, s, :], in0=xb[:, :],
                                                                          scalar=rs[:, s:s + 1], in1=gb[:, :],
                                                                          op0=ALU.mult, op1=ALU.mult),
                     R=[xb, rs, gb], W=[hn])

        def down(i):
            n = 0

            def ld(s):
                P.dma("sync", xr[s % 2][:, :], rows(src, i, s), xr[s % 2], W=[xr[s % 2]])
            ld(0)
            ld(1)
            for s in range(NSUB):
                xb = xr[s % 2]
                for hf in range(2):
                    pb = pdn[n % 2]
                    n += 1
                    self.mm_group(pb, (slice(None), slice(0, 512)),
                                  [(actT[c][:, s * 128:(s + 1) * 128], wd[:, c, hf * 512:(hf + 1) * 512])
                                   for c in range(FC)], R=wdb + actT)
                    P.op("vector", lambda e, pb=pb, hf=hf, xb=xb: e.scalar_tensor_tensor(
                        out=xb[:, hf * 512:(hf + 1) * 512], in0=pb[:, :], scalar=0.5,
                        in1=xb[:, hf * 512:(hf + 1) * 512], op0=ALU.mult, op1=ALU.add), R=[pb, xb], W=[xb])
                P.dma("sync", rows(dst, i, s), xb[:, :], xb, R=[xb])
                if s + 2 < NSUB:
                    ld(s + 2)

        norm_tile(0)
        self.transposes(hn, hT, ptr)
        for i in range(NT):
            for c in range(FC):
                pg_, pu_ = pgu[c % 2]
                self.mm_group(pg_, (slice(None), slice(0, T)),
                              [(wgu[:, kc, c * 128:(c + 1) * 128], hT[:, kc, :]) for kc in range(KC)],
                              R=[gblk[c // 4], hT])
                self.mm_group(pu_, (slice(None), slice(0, T)),
                              [(wgu[:, kc, DFF + c * 128: DFF + (c + 1) * 128], hT[:, kc, :]) for kc in range(KC)],
                              R=[ublk[c // 4], hT])
                sgb = sg[c % 2]
                P.op("scalar", lambda e, pg_=pg_, sgb=sgb: e.activation(out=sgb[:, :], in_=pg_[:, :], func=AF.Silu),
                     R=[pg_], W=[sgb])
                P.op("vector", lambda e, pu_=pu_, sgb=sgb, c=c: e.tensor_tensor(out=actT[c][:, :], in0=pu_[:, :],
                                                                              in1=sgb[:, :], op=ALU.mult),
                     R=[pu_, sgb], W=[actT[c]])
                if c == 6 and i + 1 < NT:
                    norm_tile(i + 1)
            if i + 1 < NT:
                self.transposes(hn, hT, ptr)
            down(i)
        self.end_phase()

    def phase_mix_a(self, l, src, dst):
        P = self.P
        self.begin_phase()
        NT = self.NT
        win = self.sb("win", [128, KC, 3584], BF16)
        wout = self.sb("wout", [128, 12, D], BF16)
        gb = self.sb("gb", [128, D], F32)
        cw = self.sb("cw", [128, 3, KC], F32)
        self.load_bcast(gb, self.d["mix_norm"][l], D)
        for k in range(3):
            P.dma("sync", cw[:, k, :], self.d["a_conv_w"][l, k].rearrange("(c p) -> p c", p=128), cw, W=[cw],
                  noncontig=True)
        self.load_w(win, self.d["a_w_in"][l], KC, chunk=2)
        self.load_w(wout, self.d["a_w_out"][l], 12, chunk=6)
        pg = [self.ps("pg%d" % k, [128, T], F32) for k in range(6)]
        ptr = [self.ps("ptr0", [128, 1024], BF16), self.ps("ptr1", [128, 1024], BF16)]
        mkT, mv = self.mkv(l, ptr, pg)
        xts = [self.sb("xt", [128, NSUB, D], F32) for _ in range(2)]
        hn = self.sb("hn", [128, NSUB, D], BF16)
        hT = self.sb("hT", [128, KC, T], BF16)
        st = self.norm_state()
        yT = [self.sb("yT", [128, T], BF16) for _ in range(12)]
        gcs = [self.sb("gcs", [128, T], F32) for _ in range(2)]
        vb = [self.sb("vb", [128, T + 2], F32) for _ in range(2)]
        acc = [self.sb("acc", [128, T], F32) for _ in range(2)]
        halo = [self.sb("halo", [128, 2], F32) for _ in range(KC)]
        vbh = [Buf(v.t) for v in vb]
        qmT = [self.sb("qmT", [128, T], BF16) for _ in range(MH)]
        pT = [self.sb("pT", [128, T], BF16) for _ in range(2 * MH)]
        rden = [self.sb("rden", [128, T], F32) for _ in range(2)]
        for c in range(KC):
            P.op("gpsimd", lambda e, c=c: e.memset(halo[c][:, :], 0.0), W=[halo[c]])

        self.load_x(xts[0], src, 0)
        self.norm(xts[0], gb, hn, st)
        self.transposes(hn, hT, ptr)
        for i in range(NT):
            xt = xts[i % 2]
            if i + 1 < NT:
                self.load_x(xts[(i + 1) % 2], src, i + 1)
            for c in range(KC):
                if c == 4 and i + 1 < NT:
                    self.norm(xts[(i + 1) % 2], gb, hn, st)
                pc, pu, pb = pg[(c % 2) * 3], pg[(c % 2) * 3 + 1], pg[(c % 2) * 3 + 2]
                for (pp, col) in ((pc, 1024 + c * 128), (pu, 2048 + c * 128), (pb, c * 128)):
                    self.mm_group(pp, (slice(None), slice(0, T)),
                                  [(win[:, kc, col:col + 128], hT[:, kc, :]) for kc in range(KC)], R=[win, hT])
                g_, v_, vh_, a_ = gcs[c % 2], vb[c % 2], vbh[c % 2], acc[c % 2]
                P.op("scalar", lambda e, pc=pc, g_=g_: e.copy(out=g_[:, :], in_=pc[:, :]), R=[pc], W=[g_])
                P.op("gpsimd", lambda e, v_=v_, c=c: e.tensor_copy(out=v_[:, 0:2], in_=halo[c][:, :]),
                     R=[halo[c]], W=[vh_])
                P.op("vector", lambda e, pu=pu, g_=g_, v_=v_: e.tensor_tensor(out=v_[:, 2:T + 2], in0=pu[:, :],
                                                                             in1=g_[:, :], op=ALU.mult),
                     R=[pu, g_], W=[v_])
                P.op("gpsimd", lambda e, v_=v_, c=c: e.tensor_copy(out=halo[c][:, :], in_=v_[:, T:T + 2]),
                     R=[v_], W=[halo[c]])
                P.op("scalar", lambda e, v_=v_, a_=a_, c=c: e.mul(out=a_[:, :], in_=v_[:, 2:T + 2],
                                                                 mul=cw[:, 2, c:c + 1]), R=[v_, cw], W=[a_])
                for k in (1, 0):
                    P.op("vector", lambda e, v_=v_, a_=a_, c=c, k=k: e.scalar_tensor_tensor(
                        out=a_[:, :], in0=v_[:, k:k + T], scalar=cw[:, k, c:c + 1], in1=a_[:, :],
                        op0=ALU.mult, op1=ALU.add), R=[v_, vh_, cw, a_], W=[a_])
                P.op("vector", lambda e, pb=pb, a_=a_, c=c: e.tensor_tensor(out=yT[c][:, :], in0=pb[:, :],
                                                                           in1=a_[:, :], op=ALU.mult),
                     R=[pb, a_], W=[yT[c]])
            self.mem_attn(hT, win, 3072, mkT, mv, pg, qmT, pT, rden, lambda m: (yT[8 + m], yT[8 + m][:, :]),
                          mid=(lambda: self.transposes(hn, hT, ptr)) if i + 1 < NT else None)
            n = 0
            for s in range(NSUB):
                for hf in range(2):
                    pb = pg[n % 2]
                    n += 1
                    self.mm_group(pb, (slice(None), slice(0, 512)),
                                  [(yT[c][:, s * 128:(s + 1) * 128], wout[:, c, hf * 512:(hf + 1) * 512])
                                   for c in range(12)], R=[wout] + yT)
                    P.op("vector", lambda e, pb=pb, s=s, hf=hf, xt=xt: e.tensor_tensor(
                        out=xt[:, s, hf * 512:(hf + 1) * 512], in0=pb[:, :],
                        in1=xt[:, s, hf * 512:(hf + 1) * 512], op=ALU.add), R=[pb, xt], W=[xt])
            self.store_x(xt, dst, i)
        self.end_phase()

    def phase_kvf(self, src):
        P = self.P
        self.begin_phase()
        NT = self.NT
        d = self.d
        w = self.sb("wkvf", [128, KC, 2056], BF16)
        gb = self.sb("gb", [128, D], F32)
        self.load_bcast(gb, d["kv_norm"], D)
        self.load_w(w, d["w_kvf"], KC, chunk=2)
        bfb = self.sb("bfb", [128, 32], F32)
        for s in range(NSUB):
            P.dma("sync", bfb[:, s * 8:(s + 1) * 8], d["b_f"].partition_broadcast(128), bfb, W=[bfb])
        xts = [self.sb("xt", [128, NSUB, D], F32) for _ in range(2)]
        hn = self.sb("hn", [128, NSUB, D], BF16)
        hT = self.sb("hT", [128, KC, T], BF16)
        st = self.norm_state()
        ptr = [self.ps("ptr0", [128, 1024], BF16), self.ps("ptr1", [128, 1024], BF16)]
        pg = [self.ps("pg%d" % k, [128, T], F32) for k in range(3)]
        pf = self.ps("pf", [128, T], F32)
        pc = self.ps("pc", [128, T], F32)
        pct = self.ps("pct", [128, T], F32)
        kst = [self.sb("kst", [128, H, T], BF16) for _ in range(2)]
        vst = [self.sb("vst", [128, NSUB, D], BF16) for _ in range(2)]
        fl = self.sb("fl", [128, 32], F32)
        ex = self.sb("ex", [128, 32], F32)
        sp = self.sb("sp", [128, 32], F32)
        gprev = self.sb("gprev", [128, 8], F32)
        csb = self.sb("csb", [128, 32], F32)
        cst = [self.sb("cst", [8, T], F32) for _ in range(2)]
        ntri, nones, identf = self.ntri_f, self.nones_f, self.ident_f
        negc = self.negc
        P.op("gpsimd", lambda e: e.memset(gprev[:, :], 0.0), W=[gprev])
        kT_v = d["kT"].rearrange("h p s -> p h s")
        vP_v = d["vP"].rearrange("h p c e -> p c h e")

        self.load_x(xts[0], src, 0)
        for i in range(NT):
            xt = xts[i % 2]
            if i + 1 < NT:
                self.load_x(xts[(i + 1) % 2], src, i + 1)
            self.norm(xt, gb, hn, st)
            self.transposes(hn, hT, ptr)
            ks, vs, cs = kst[i % 2], vst[i % 2], cst[i % 2]
            for h in range(H):
                pb = pg[h % 3]
                self.mm_group(pb, (slice(None), slice(0, T)),
                              [(w[:, kc, h * 128:(h + 1) * 128], hT[:, kc, :]) for kc in range(KC)], R=[w, hT])
                if h % 2 == 0:
                    P.op("scalar", lambda e, pb=pb, h=h, ks=ks: e.copy(out=ks[:, h, :], in_=pb[:, :]), R=[pb], W=[ks])
                else:
                    P.op("vector", lambda e, pb=pb, h=h, ks=ks: e.tensor_copy(out=ks[:, h, :], in_=pb[:, :]),
                         R=[pb], W=[ks])
            P.dma("sync", kT_v[:, :, i * T:(i + 1) * T], ks[:, :, :], ks, R=[ks])
            n = 0
            for s in range(NSUB):
                for hf in range(2):
                    pb = pg[n % 3]
                    n += 1
                    self.mm_group(pb, (slice(None), slice(0, 512)),
                                  [(hT[:, kc, s * 128:(s + 1) * 128], w[:, kc, 1024 + hf * 512: 1024 + (hf + 1) * 512])
                                   for kc in range(KC)], R=[w, hT])
                    if n % 2 == 0:
                        P.op("scalar", lambda e, pb=pb, s=s, hf=hf, vs=vs: e.copy(
                            out=vs[:, s, hf * 512:(hf + 1) * 512], in_=pb[:, :]), R=[pb], W=[vs])
                    else:
                        P.op("vector", lambda e, pb=pb, s=s, hf=hf, vs=vs: e.tensor_copy(
                            out=vs[:, s, hf * 512:(hf + 1) * 512], in_=pb[:, :]), R=[pb], W=[vs])
            for s in range(NSUB):
                P.dma("sync", vP_v[:, i * NSUB + s, :, :],
                      vs[:, s, :].rearrange("p (h e) -> p h e", h=H), vs, R=[vs])
            for s in range(NSUB):
                self.mm_group(pf, (slice(None), slice(s * 8, (s + 1) * 8)),
                              [(hT[:, kc, s * 128:(s + 1) * 128], w[:, kc, 2048:2056]) for kc in range(KC)], R=[w, hT])
            P.op("vector", lambda e: e.tensor_tensor(out=fl[:, :], in0=pf[:, 0:32], in1=bfb[:, :], op=ALU.add),
                 R=[pf, bfb], W=[fl])
            P.op("scalar", lambda e: e.activation(out=ex[:, :], in_=fl[:, :], func=AF.Exp, scale=-1.0), R=[fl], W=[ex])
            P.op("scalar", lambda e: e.activation(out=sp[:, :], in_=ex[:, :], func=AF.Ln, bias=1.0), R=[ex], W=[sp])
            for s in range(NSUB):
                self.mm_group(pc, (slice(None), slice(s * 8, (s + 1) * 8)),
                              [(ntri[:, :], sp[:, s * 8:(s + 1) * 8]), (nones[:, :], gprev[:, :])],
                              R=[ntri, nones, sp, gprev])
                P.op("vector", lambda e, s=s: e.tensor_tensor(out=gprev[:, :], in0=gprev[:, :],
                                                              in1=sp[:, s * 8:(s + 1) * 8], op=ALU.add),
                     R=[sp, gprev], W=[gprev])
            P.op("vector", lambda e: e.tensor_copy(out=csb[:, :], in_=pc[:, 0:32]), R=[pc], W=[csb])
            P.op("gpsimd", lambda e, i=i: e.tensor_scalar_mul(out=negc[:, i * NSUB:(i + 1) * NSUB, :],
                                                              in0=csb[:, :].rearrange("p (s h) -> p s h", h=H),
                                                              scalar1=-1.0), R=[csb], W=[negc])
            for s in range(NSUB):
                P.op("tensor", lambda e, s=s: e.transpose(out=pct[0:8, s * 128:(s + 1) * 128],
                                                          in_=csb[:, s * 8:(s + 1) * 8], identity=identf[:, :]),
                     R=[csb, identf], W=[pct], signal=(s == NSUB - 1))
            P.op("vector", lambda e, cs=cs: e.tensor_copy(out=cs[:, :], in_=pct[0:8, :]), R=[pct], W=[cs])
            P.dma("sync", d["cT"][:, i * T:(i + 1) * T], cs[:, :], cs, R=[cs])
        self.end_phase()

    def phase_b1(self, l, src):
        P = self.P
        self.begin_phase()
        NT = self.NT
        d = self.d
        jl = l - N_A
        wq = self.sb("wq", [128, KC, 1536], BF16)
        gb = self.sb("gb", [128, D], F32)
        self.load_bcast(gb, d["mix_norm"][l], D)
        self.load_w(wq, d["b_w_q"][jl], KC, chunk=4)
        pg = [self.ps("pg%d" % k, [128, T], F32) for k in range(6)]
        ptr = [self.ps("ptr0", [128, 1024], BF16), self.ps("ptr1", [128, 1024], BF16)]
        mkT, mv = self.mkv(l, ptr, pg)
        xts = [self.sb("xt", [128, NSUB, D], F32) for _ in range(2)]
        hn = self.sb("hn", [128, NSUB, D], BF16)
        hT = self.sb("hT", [128, KC, T], BF16)
        st = self.norm_state()
        qst = [self.sb("qst", [128, H, T], BF16) for _ in range(2)]
        yms = [self.sb("yms", [128, MH, T], BF16) for _ in range(2)]
        qmT = [self.sb("qmT", [128, T], BF16) for _ in range(MH)]
        pT = [self.sb("pT", [128, T], BF16) for _ in range(2 * MH)]
        rden = [self.sb("rden", [128, T], F32) for _ in range(2)]
        qT_v = d["qT"].rearrange("h p s -> p h s")
        yT_v = d["yT"].rearrange("c p s -> p c s")
        self.load_x(xts[0], src, 0)
        self.norm(xts[0], gb, hn, st)
        self.transposes(hn, hT, ptr)
        for i in range(NT):
            xt = xts[i % 2]
            if i + 1 < NT:
                self.load_x(xts[(i + 1) % 2], src, i + 1)
            qs, ym = qst[i % 2], yms[i % 2]
            for h in range(H):
                if h == 3 and i + 1 < NT:
                    self.norm(xts[(i + 1) % 2], gb, hn, st)
                pb = pg[2 + h % 2]
                self.mm_group(pb, (slice(None), slice(0, T)),
                              [(wq[:, kc, h * 128:(h + 1) * 128], hT[:, kc, :]) for kc in range(KC)], R=[wq, hT])
                if h % 2 == 0:
                    P.op("scalar", lambda e, pb=pb, h=h, qs=qs: e.copy(out=qs[:, h, :], in_=pb[:, :]), R=[pb], W=[qs])
                else:
                    P.op("vector", lambda e, pb=pb, h=h, qs=qs: e.tensor_copy(out=qs[:, h, :], in_=pb[:, :]),
                         R=[pb], W=[qs])
            P.dma("sync", qT_v[:, :, i * T:(i + 1) * T], qs[:, :, :], qs, R=[qs])
            self.mem_attn(hT, wq, 1024, mkT, mv, pg, qmT, pT, rden, lambda m, ym=ym: (ym, ym[:, m, :]),
                          mid=(lambda: self.transposes(hn, hT, ptr)) if i + 1 < NT else None)
            P.dma("sync", yT_v[:, 8:12, i * T:(i + 1) * T], ym[:, :, :], ym, R=[ym])
        self.end_phase()

    def phase_b2(self):
        P = self.P
        self.begin_phase()
        NT, S, NCH = self.NT, self.S, self.NCH
        d = self.d
        LA = 3
        NPL, NTMP, NPT = 4, 4, 6
        kTh = [self.sb("kTh", [128, S], BF16) for _ in range(2)]
        vh = [self.sb("vh", [128, NCH, 128], BF16) for _ in range(2)]
        qTt = [self.sb("qTt", [128, T], BF16) for _ in range(2)]
        cb = [self.sb("cb", [128, T], F32) for _ in range(2)]
        cbm = [[self.sb("cbm", [128, T], F32) for _ in range(4)] for _ in range(2)]
        tmp = [self.sb("tmp", [128, T], F32) for _ in range(NTMP)]
        pT = [self.sb("pT", [128, T], BF16) for _ in range(NPT)]
        rden = self.sb("rden", [128, T], F32)
        yst = [self.sb("yst", [128, T], BF16) for _ in range(2)]
        pl = [self.ps("pl%d" % k, [128, T], F32) for k in range(NPL)]
        po = [self.ps("po%d" % k, [128, T], F32) for k in range(2)]
        pd = [self.ps("pd%d" % k, [128, T], F32) for k in range(2)]
        ones, negc = self.ones_b, self.negc
        maskn = self.sb("maskn", [128, 4, T], F32)
        P.dma("sync", maskn[:, :, :], d["c_mask"], maskn, W=[maskn])

        def load_head(h):
            P.dma("sync", kTh[h % 2][:, :], d["kT"][h], kTh[h % 2], W=[kTh[h % 2]])
            P.dma("sync", vh[h % 2][:, :, :], d["vP"][h], vh[h % 2], W=[vh[h % 2]])

        def load_qc(it):
            h, i = divmod(it, NT)
            q_, c_ = qTt[it % 2], cb[it % 2]
            P.dma("sync", q_[:, :], d["qT"][h, :, i * T:(i + 1) * T], q_, W=[q_])
            P.dma("sync", c_[:, :], d["cT"][h, i * T:(i + 1) * T].partition_broadcast(128), c_, W=[c_])

        blocks = []
        for it in range(H * NT):
            h, i = divmod(it, NT)
            nj = 4 * i + 4
            for j in range(nj):
                blocks.append((it, h, i, j, nj))
        NB = len(blocks)

        def stage1(n):
            it, h, i, j, nj = blocks[n]
            kb = kTh[h % 2]
            q_, c_, cm_ = qTt[it % 2], cb[it % 2], cbm[it % 2]
            if j == 0:
                for jj in range(4):
                    P.op("gpsimd", lambda e, jj=jj, c_=c_, cm_=cm_: e.tensor_tensor(
                        out=cm_[jj][:, :], in0=c_[:, :], in1=maskn[:, jj, :], op=ALU.add),
                        R=[c_, maskn], W=[cm_[jj]])
                if it + 1 < H * NT:
                    load_qc(it + 1)
                if h + 1 < H and ((NT > 1 and i == 1) or (NT == 1 and i == 0)):
                    load_head(h + 1)
            jj = j - 4 * i
            bias_t = c_ if jj < 0 else cm_[jj]
            l_, t_, p_ = pl[n % NPL], tmp[n % NTMP], pT[n % NPT]
            self.mm_group(l_, (slice(None), slice(0, T)), [(kb[:, j * 128:(j + 1) * 128], q_[:, :])], R=[kb, q_])
            P.op("vector", lambda e, l_=l_, t_=t_, bias_t=bias_t: e.scalar_tensor_tensor(
                out=t_[:, :], in0=l_[:, :], scalar=SCALE, in1=bias_t[:, :], op0=ALU.mult, op1=ALU.add),
                R=[l_, bias_t], W=[t_])
            P.op("scalar", lambda e, t_=t_, p_=p_, j=j, h=h: e.activation(
                out=p_[:, :], in_=t_[:, :], func=AF.Exp, bias=negc[:, j, h:h + 1]), R=[t_, negc], W=[p_])

        def stage2(n):
            it, h, i, j, nj = blocks[n]
            vb_ = vh[h % 2]
            o_, d_, y_ = po[it % 2], pd[it % 2], yst[it % 2]
            p_ = pT[n % NPT]
            P.op("tensor", lambda e, o_=o_, p_=p_, j=j, vb_=vb_, nj=nj: e.matmul(
                o_[:, :], lhsT=vb_[:, j, :], rhs=p_[:, :], start=(j == 0), stop=(j == nj - 1)),
                R=[vb_, p_], W=[o_], signal=False)
            P.op("tensor", lambda e, d_=d_, p_=p_, j=j, nj=nj: e.matmul(
                d_[:, :], lhsT=ones[:, :], rhs=p_[:, :], start=(j == 0), stop=(j == nj - 1)),
                R=[ones, p_], W=[d_], signal=True)
            if j == nj - 1:
                P.op("vector", lambda e, d_=d_: e.reciprocal(out=rden[:, :], in_=d_[:, :]), R=[d_], W=[rden])
                P.op("vector", lambda e, o_=o_, y_=y_: e.tensor_tensor(out=y_[:, :], in0=o_[:, :], in1=rden[:, :],
                                                                      op=ALU.mult), R=[o_, rden], W=[y_])
                P.dma("sync", d["yT"][h, :, i * T:(i + 1) * T], y_[:, :], y_, R=[y_])

        load_head(0)
        load_qc(0)
        if NT == 1 and H > 1:
            pass
        for n in range(NB + LA):
            if n < NB:
                stage1(n)
            if n >= LA:
                stage2(n - LA)
        self.end_phase()

    def phase_b3(self, l, src, dst):
        P = self.P
        self.begin_phase()
        NT = self.NT
        d = self.d
        jl = l - N_A
        wout = self.sb("wout", [128, 12, D], BF16)
        self.load_w(wout, d["b_w_out"][jl], 12, chunk=6)
        xts = [self.sb("xt", [128, NSUB, D], F32) for _ in range(2)]
        yTt = [self.sb("yTt", [128, 12, T], BF16) for _ in range(2)]
        pg = [self.ps("pg%d" % k, [128, T], F32) for k in range(2)]
        yT_v = d["yT"].rearrange("c p s -> p c s")

        def loads(i):
            self.load_x(xts[i % 2], src, i)
            P.dma("sync", yTt[i % 2][:, :, :], yT_v[:, :, i * T:(i + 1) * T], yTt[i % 2], W=[yTt[i % 2]])
        loads(0)
        for i in range(NT):
            xt, yt = xts[i % 2], yTt[i % 2]
            if i + 1 < NT:
                loads(i + 1)
            n = 0
            for s in range(NSUB):
                for hf in range(2):
                    pb = pg[n % 2]
                    n += 1
                    self.mm_group(pb, (slice(None), slice(0, 512)),
                                  [(yt[:, c, s * 128:(s + 1) * 128], wout[:, c, hf * 512:(hf + 1) * 512])
                                   for c in range(12)], R=[wout, yt])
                    P.op("vector", lambda e, pb=pb, s=s, hf=hf, xt=xt: e.tensor_tensor(
                        out=xt[:, s, hf * 512:(hf + 1) * 512], in0=pb[:, :],
                        in1=xt[:, s, hf * 512:(hf + 1) * 512], op=ALU.add), R=[pb, xt], W=[xt])
            self.store_x(xt, dst, i)
        self.end_phase()

    def phase_final(self, src, dst):
        P = self.P
        self.begin_phase()
        NT = self.NT
        gb = self.sb("gb", [128, D], F32)
        self.load_bcast(gb, self.d["final_norm"], D)
        xts = [self.sb("xt", [128, NSUB, D], F32) for _ in range(2)]
        ots = [self.sb("ot", [128, NSUB, D], F32) for _ in range(2)]
        st = self.norm_state()
        self.load_x(xts[0], src, 0)
        for i in range(NT):
            if i + 1 < NT:
                self.load_x(xts[(i + 1) % 2], src, i + 1)
            self.norm(xts[i % 2], gb, ots[i % 2], st)
            self.store_x(ots[i % 2], dst, i)
        self.end_phase()

    def phase_copy(self, src, dst):
        self.begin_phase()
        xts = [self.sb("xt", [128, NSUB, D], F32) for _ in range(2)]
        for i in range(self.NT):
            self.load_x(xts[i % 2], src, i)
            self.store_x(xts[i % 2], dst, i)
        self.end_phase()

    def build(self):
        nc = self.nc
        S = self.S
        self.uid = 0

        def din(name, shape):
            return nc.dram_tensor(name, list(shape), F32, kind="ExternalInput").ap()
        d = {}
        d["x"] = din("x", [S, D])
        d["mem"] = din("mem", [M, D])
        d["ffn_norm"] = din("ffn_norm", [DEPTH, 2, D])
        d["ffn_w_gate_up"] = din("ffn_w_gate_up", [DEPTH, 2, D, 2 * DFF])
        d["ffn_w_down"] = din("ffn_w_down", [DEPTH, 2, DFF, D])
        d["mix_norm"] = din("mix_norm", [DEPTH, D])
        d["mem_norm"] = din("mem_norm", [D])
        d["mem_w_kv"] = din("mem_w_kv", [DEPTH, D, 1024])
        d["a_w_in"] = din("a_w_in", [N_A, D, 3584])
        d["a_conv_w"] = din("a_conv_w", [N_A, 3, D])
        d["a_w_out"] = din("a_w_out", [N_A, 1536, D])
        d["kv_norm"] = din("kv_norm", [D])
        d["w_kvf"] = din("w_kvf", [D, 2056])
        d["b_f"] = din("b_f", [H])
        d["b_w_q"] = din("b_w_q", [DEPTH - N_A, D, 1536])
        d["b_w_out"] = din("b_w_out", [DEPTH - N_A, 1536, D])
        d["final_norm"] = din("final_norm", [D])
        d["c_ident"] = din("c_ident", [128, 128])
        d["c_ntri"] = din("c_ntri", [128, 128])
        d["c_mask"] = din("c_mask", [128, 4, T])
        d["out"] = nc.dram_tensor("out", [S, D], F32, kind="ExternalOutput").ap()
        d["xs"] = nc.dram_tensor("xs", [S, D], F32).ap()
        d["kT"] = nc.dram_tensor("kT", [H, 128, S], BF16).ap()
        d["vP"] = nc.dram_tensor("vP", [H, 128, S // 128, 128], BF16).ap()
        d["cT"] = nc.dram_tensor("cT", [H, S], F32).ap()
        d["qT"] = nc.dram_tensor("qT", [H, 128, S], BF16).ap()
        d["yT"] = nc.dram_tensor("yT", [12, 128, S], BF16).ap()
        self.d = d

        with ExitStack() as ges:
            self.ges = ges
            self.P = P = Prog(nc, ges)
            self.ident_f = self.gsb("ident_f", [128, 128], F32)
            self.ident_b = self.gsb("ident_b", [128, 128], BF16)
            self.ntri_f = self.gsb("ntri_f", [128, 128], F32)
            self.nones_f = self.gsb("nones_f", [128, 128], F32)
            self.ones_b = self.gsb("ones_b", [128, 128], BF16)
            self.memT = self.gsb("memT", [128, KC, M], BF16)
            self.negc = self.gsb("negc", [128, S // 128, H], F32)
            self.begin_phase()
            P.dma("sync", self.ident_f[:, :], d["c_ident"], self.ident_f, W=[self.ident_f])
            P.dma("gpsimd", self.ident_b[:, :], d["c_ident"], self.ident_b, W=[self.ident_b])
            P.dma("sync", self.ntri_f[:, :], d["c_ntri"], self.ntri_f, W=[self.ntri_f])
            P.op("gpsimd", lambda e: e.memset(self.nones_f[:, :], -1.0), W=[self.nones_f])
            P.op("gpsimd", lambda e: e.memset(self.ones_b[:, :], 1.0), W=[self.ones_b])
            self.end_phase()

            phases = []
            phases.append(lambda: self.phase_pre())
            for l in range(DEPTH):
                src0 = d["x"] if l == 0 else d["xs"]
                if l == N_A:
                    phases.append(lambda: self.phase_kvf(d["xs"]))
                phases.append(lambda l=l, src0=src0: self.phase_ffn(l, 0, src0, d["xs"]))
                if l < N_A:
                    phases.append(lambda l=l: self.phase_mix_a(l, d["xs"], d["xs"]))
                else:
                    phases.append(lambda l=l: self.phase_b1(l, d["xs"]))
                    phases.append(lambda: self.phase_b2())
                    phases.append(lambda l=l: self.phase_b3(l, d["xs"], d["xs"]))
                phases.append(lambda l=l: self.phase_ffn(l, 1, d["xs"], d["xs"]))
            phases.append(lambda: self.phase_final(d["xs"], d["out"]))
            if self.stop_after is not None:
                phases = phases[:self.stop_after]
                phases.append(lambda: self.phase_copy(d["xs"], d["out"]))
            for ph in phases:
                ph()
        return nc


def _consts():
    ident = np.eye(128, dtype=np.float32)
    s = np.arange(128)[:, None]
    t = np.arange(128)[None, :]
    ntri = np.where(s <= t, -1.0, 0.0).astype(np.float32)
    tt = np.arange(T)[None, None, :]
    jj = np.arange(4)[None, :, None]
    mask = np.where(jj * 128 + s[:, :, None] <= tt, 0.0, MASKV).astype(np.float32)
    return {"c_ident": ident, "c_ntri": ntri, "c_mask": np.ascontiguousarray(mask)}


_NC_CACHE = {}


def run(inputs, S, n_cores, stop_after=None):
    key = (S, stop_after)
    if key not in _NC_CACHE:
        _NC_CACHE[key] = Builder(S, stop_after).build()
    nc = _NC_CACHE[key]
    consts = _consts()
    shared = {k: np.ascontiguousarray(np.asarray(v, dtype=np.float32)) for k, v in inputs.items()
              if k not in ("x", "mem")}
    shared.update(consts)
    x = np.asarray(inputs["x"], dtype=np.float32)
    mem = np.asarray(inputs["mem"], dtype=np.float32)
    in_maps = []
    for b in range(n_cores):
        m = dict(shared)
        m["x"] = np.ascontiguousarray(x[b])
        m["mem"] = np.ascontiguousarray(mem[b])
        in_maps.append(m)
    res = run_bass_kernel_spmd(nc, in_maps, core_ids=list(range(n_cores)))
    return np.stack([res.results[b]["out"] for b in range(n_cores)], axis=0)


def kernel(**inputs):
    x = inputs["x"]
    B, S, _ = x.shape
    return run(inputs, S, B).astype(np.float32)
```
